# Optimizing a Trainium2 kernel written in Bass

```python
import jax, jax.numpy as jnp
from jax import lax
import numpy as np

D_MODEL = 2048
BATCH = 16
SEQ = 256
DEPTH = 2
DEC_BATCH = 8
DEC_SEQ = 1024
PAST_LEN = 512

GRID_W = 64
QBLOCK = 128
ROPE_BASE = 10000.0
RMS_EPS = 1e-6
LN_EPS = 1e-5
NEG_INF = -1e30

N_EVEN = (DEPTH + 1) // 2
N_ODD = DEPTH // 2

MLA_HEADS = 8
Q_LORA = 512
KV_LORA = 256
QK_NOPE = 128
QK_ROPE = 64
V_HEAD = 128
MLA_OUT = MLA_HEADS * V_HEAD
POOL_WINDOWS = (2, 4, 8, 16)
POOL_GROUPS = len(POOL_WINDOWS)
POOL_CH = D_MODEL // 2
POOL_GC = POOL_CH // POOL_GROUPS
EVEN_IN = Q_LORA + KV_LORA + QK_ROPE + POOL_CH
EVEN_MIX = MLA_OUT + POOL_CH
SWA_HEADS = 32
SWA_KV = 4
SWA_HD = 64
SWA_GROUP = SWA_HEADS // SWA_KV
WINDOW = 128
ODD_MIX = SWA_HEADS * SWA_HD
ODD_IN = ODD_MIX + 2 * SWA_KV * SWA_HD
N_EXPERTS = 16
EXPERT_FF = 2048
CAP_FACTOR = 2
DN_ALPHA = (2 * DEPTH) ** 0.25
DN_BETA = (8 * DEPTH) ** -0.25

kernel_name = 'hybrid_diffusion_mla_pool_swa_ecmoe_step'

f32 = jnp.float32


def rms_norm(x, g):
    xf = x.astype(f32)
    y = xf * lax.rsqrt(jnp.mean(xf * xf, -1, keepdims=True) + RMS_EPS)
    return (y * g.astype(f32)).astype(x.dtype)


def layer_norm(x, g, b):
    xf = x.astype(f32)
    xc = xf - jnp.mean(xf, -1, keepdims=True)
    var = jnp.mean(xc * xc, -1, keepdims=True)
    return (xc * lax.rsqrt(var + LN_EPS) * g.astype(f32) + b.astype(f32)).astype(x.dtype)


def axial_rope_angles(n_tokens, rot_dim):
    rows = n_tokens // GRID_W
    row = jnp.repeat(jnp.arange(rows, dtype=f32), GRID_W)
    col = jnp.tile(jnp.arange(GRID_W, dtype=f32), rows)
    n_freq = rot_dim // 4
    inv = ROPE_BASE ** (-jnp.arange(n_freq, dtype=f32) / n_freq)
    ang = jnp.concatenate([row[:, None] * inv, col[:, None] * inv], -1)
    return jnp.cos(ang), jnp.sin(ang)


def apply_rope(x, cos, sin):
    shape = (1, x.shape[1]) + (1,) * (x.ndim - 3) + (cos.shape[-1],)
    c, s = cos.reshape(shape), sin.reshape(shape)
    xf = x.astype(f32)
    x1, x2 = xf[..., 0::2], xf[..., 1::2]
    out = jnp.stack([x1 * c - x2 * s, x1 * s + x2 * c], -1).reshape(x.shape)
    return out.astype(x.dtype)


def dense_attention(q, k, v, sink=None):
    b, sq, hk, g, dk = q.shape
    scale = dk ** -0.5
    nb = sq // QBLOCK
    qb = jnp.moveaxis(q.reshape(b, nb, QBLOCK, hk, g, dk), 1, 0)

    def block(qi):
        s = jnp.einsum('bqhgd,bkhd->bhgqk', qi, k, preferred_element_type=f32) * scale
        if sink is None:
            p = jax.nn.softmax(s, -1)
        else:
            s_sink = jnp.broadcast_to(sink.astype(f32)[None, :, :, None, None], s.shape[:-1] + (1,))
            p = jax.nn.softmax(jnp.concatenate([s, s_sink], -1), -1)[..., :-1]
        return jnp.einsum('bhgqk,bkhe->bqhge', p.astype(v.dtype), v)

    out = lax.map(block, qb)
    return jnp.moveaxis(out, 0, 1).reshape(b, sq, hk, g, v.shape[-1])


def window_attention(q, k, v, k_ctx, v_ctx, sink):
    b, s_len, hk, g, d = q.shape
    nb = s_len // QBLOCK
    n_ctx = k_ctx.shape[1]
    scale = d ** -0.5
    pad = ((0, 0), (QBLOCK, QBLOCK), (0, 0), (0, 0))

    def band(t):
        tb = jnp.pad(t, pad).reshape(b, nb + 2, QBLOCK, hk, t.shape[-1])
        return jnp.moveaxis(jnp.concatenate([tb[:, :-2], tb[:, 1:-1], tb[:, 2:]], 2), 1, 0)

    kb, vb = band(k), band(v)
    qb = jnp.moveaxis(q.reshape(b, nb, QBLOCK, hk, g, d), 1, 0)
    k_off = jnp.arange(3 * QBLOCK) - QBLOCK
    rel = k_off[None, :] - jnp.arange(QBLOCK)[:, None]
    in_window = jnp.abs(rel) <= WINDOW
    sink_f = sink.astype(f32)

    def block(args):
        i, qi, ki, vi = args
        kpos = i * QBLOCK + k_off
        valid = in_window & ((kpos >= 0) & (kpos < s_len))[None, :]
        s_loc = jnp.einsum('bqhgd,bkhd->bhgqk', qi, ki, preferred_element_type=f32) * scale
        s_loc = jnp.where(valid, s_loc, NEG_INF)
        s_ctx = jnp.einsum('bqhgd,bkhd->bhgqk', qi, k_ctx, preferred_element_type=f32) * scale
        s_sink = jnp.broadcast_to(sink_f[None, :, :, None, None], s_loc.shape[:-1] + (1,))
        p = jax.nn.softmax(jnp.concatenate([s_loc, s_ctx, s_sink], -1), -1)
        p_loc = p[..., :3 * QBLOCK].astype(vi.dtype)
        p_ctx = p[..., 3 * QBLOCK:3 * QBLOCK + n_ctx].astype(v_ctx.dtype)
        return (jnp.einsum('bhgqk,bkhe->bqhge', p_loc, vi)
                + jnp.einsum('bhgqk,bkhe->bqhge', p_ctx, v_ctx))

    out = lax.map(block, (jnp.arange(nb), qb, kb, vb))
    return jnp.moveaxis(out, 0, 1).reshape(b, s_len, hk, g, d)


def pool_mixer(u, w_pool, pool_scale):
    b, s_len, _ = u.shape
    uf = u.astype(f32).reshape(b, s_len, POOL_GROUPS, POOL_GC)
    cs = jnp.pad(jnp.cumsum(uf, axis=1), ((0, 0), (1, 0), (0, 0), (0, 0)))
    t = jnp.arange(s_len)
    outs = []
    for gi, w in enumerate(POOL_WINDOWS):
        lo = jnp.clip(t - w // 2, 0, s_len)
        hi = jnp.clip(t + w // 2, 0, s_len)
        cnt = (hi - lo).astype(f32)[None, :, None]
        csg = cs[:, :, gi]
        outs.append((csg[:, hi] - csg[:, lo]) / cnt - uf[:, :, gi])
    pooled = jnp.stack(outs, 2).astype(u.dtype)
    mixed = jnp.einsum('bsgc,gce->bsge', pooled, w_pool)
    return mixed.reshape(b, s_len, POOL_CH) * pool_scale


def mla_keys(c_kv, k_pe, w_ukv):
    b, s_len, _ = c_kv.shape
    kv = (c_kv @ w_ukv).reshape(b, s_len, MLA_HEADS, QK_NOPE + V_HEAD)
    k_pe_h = jnp.broadcast_to(k_pe[:, :, None, :], (b, s_len, MLA_HEADS, QK_ROPE))
    return jnp.concatenate([kv[..., :QK_NOPE], k_pe_h], -1), kv[..., QK_NOPE:]


def even_mixer(h, e, P, ctx):
    b, s_len, _ = h.shape
    proj = h @ P['w_in_even'][e]
    c_q, c_kv, k_pe, u_pool = jnp.split(proj, [Q_LORA, Q_LORA + KV_LORA, Q_LORA + KV_LORA + QK_ROPE], axis=-1)
    c_q = rms_norm(c_q, P['q_norm'][e])
    c_kv = rms_norm(c_kv, P['kv_norm'][e])
    q = (c_q @ P['w_uq'][e]).reshape(b, s_len, MLA_HEADS, QK_NOPE + QK_ROPE)
    w_ukv = P['w_ukv'][e]
    if ctx is None:
        k, v = mla_keys(c_kv, k_pe, w_ukv)
        new = (c_kv, k_pe)
    else:
        cos, sin = axial_rope_angles(s_len, QK_ROPE)
        q = jnp.concatenate([q[..., :QK_NOPE], apply_rope(q[..., QK_NOPE:], cos, sin)], -1)
        k_pe_rot = apply_rope(k_pe[:, :, None, :], cos, sin)[:, :, 0, :]
        k_lat, v_lat = mla_keys(c_kv, k_pe_rot, w_ukv)
        k_ctx, v_ctx = mla_keys(ctx[0], ctx[1], w_ukv)
        k = jnp.concatenate([k_lat, k_ctx], 1)
        v = jnp.concatenate([v_lat, v_ctx], 1)
        new = None
    attn = dense_attention(q[:, :, :, None, :], k, v)
    pooled = pool_mixer(u_pool, P['w_pool'][e], P['pool_scale'][e])
    mix = jnp.concatenate([attn.reshape(b, s_len, MLA_OUT), pooled], -1) @ P['w_out_even'][e]
    return mix, new


def odd_mixer(h, o, P, ctx):
    b, s_len, _ = h.shape
    proj = h @ P['w_in_odd'][o]
    q, k, v = jnp.split(proj, [ODD_MIX, ODD_MIX + SWA_KV * SWA_HD], axis=-1)
    q = q.reshape(b, s_len, SWA_KV, SWA_GROUP, SWA_HD)
    k = k.reshape(b, s_len, SWA_KV, SWA_HD)
    v = v.reshape(b, s_len, SWA_KV, SWA_HD)
    sink = P['sink'][o].reshape(SWA_KV, SWA_GROUP)
    if ctx is None:
        attn = dense_attention(q, k, v, sink)
        new = (k, v)
    else:
        cos, sin = axial_rope_angles(s_len, SWA_HD)
        attn = window_attention(apply_rope(q, cos, sin), apply_rope(k, cos, sin), v, ctx[0], ctx[1], sink)
        new = None
    return attn.reshape(b, s_len, ODD_MIX) @ P['w_out_odd'][o], new


def ec_moe(x, w_router, w_gate, w_up, w_down):
    b, n, _ = x.shape
    cap = max(1, CAP_FACTOR * n // N_EXPERTS)
    aff = jax.nn.softmax(jnp.einsum('bnd,de->bne', x, w_router, preferred_element_type=f32), -1)
    top_v, top_i = lax.top_k(jnp.swapaxes(aff, 1, 2), cap)
    b_idx = jnp.arange(b)[:, None, None]
    xs = x[b_idx, top_i]
    hid = jax.nn.silu(jnp.einsum('becd,edf->becf', xs, w_gate)) * jnp.einsum('becd,edf->becf', xs, w_up)
    y = jnp.einsum('becf,efd->becd', hid, w_down) * top_v[..., None].astype(x.dtype)
    return jnp.zeros_like(x).at[b_idx, top_i].add(y.astype(x.dtype))


def modulation(cond, w_ada, b_ada):
    m = (jax.nn.silu(cond) @ w_ada + b_ada)[:, None, :]
    return jnp.split(m, 6, axis=-1)


def run_trunk(x, cond, P, ctx_caches):
    ckv_l, kpe_l, k_l, v_l = [], [], [], []
    for l in range(DEPTH):
        sh1, sc1, g1, sh2, sc2, g2 = modulation(cond, P['w_ada'][l], P['b_ada'][l])
        h = x * (1 + sc1) + sh1
        if l % 2 == 0:
            e = l // 2
            ctx = None if ctx_caches is None else (ctx_caches[0][:, e], ctx_caches[1][:, e])
            mix, new = even_mixer(h, e, P, ctx)
            if new is not None:
                ckv_l.append(new[0])
                kpe_l.append(new[1])
        else:
            o = l // 2
            ctx = None if ctx_caches is None else (ctx_caches[2][:, o], ctx_caches[3][:, o])
            mix, new = odd_mixer(h, o, P, ctx)
            if new is not None:
                k_l.append(new[0])
                v_l.append(new[1])
        x = layer_norm(DN_ALPHA * x + g1 * mix, P['ln1_g'][l], P['ln1_b'][l])
        h = x * (1 + sc2) + sh2
        ffn = ec_moe(h, P['w_router'][l], P['w_gate'][l], P['w_up'][l], P['w_down'][l])
        x = layer_norm(DN_ALPHA * x + g2 * ffn, P['ln2_g'][l], P['ln2_b'][l])
    return x, (ckv_l, kpe_l, k_l, v_l)


def setup_inputs(seed: int = 0) -> dict:
    key = jax.random.key(seed)
    ks = iter(jax.random.split(key, 40))

    def nrm(shape, scale=1.0):
        return jax.random.normal(next(ks), shape, jnp.float32) * scale

    d = D_MODEL
    return {
        'x_prompt': nrm((BATCH, SEQ, d)),
        'x_sample': nrm((DEC_BATCH, DEC_SEQ, d)),
        'cache_mla_ckv': nrm((DEC_BATCH, N_EVEN, PAST_LEN, KV_LORA)),
        'cache_mla_kpe': nrm((DEC_BATCH, N_EVEN, PAST_LEN, QK_ROPE)),
        'cache_swa_k': nrm((DEC_BATCH, N_ODD, PAST_LEN, SWA_KV, SWA_HD)),
        'cache_swa_v': nrm((DEC_BATCH, N_ODD, PAST_LEN, SWA_KV, SWA_HD)),
        'c': nrm((DEC_BATCH, d)),
        'c_ctx': nrm((d,)),
        'w_ada': nrm((DEPTH, d, 6 * d), 0.5 * d ** -0.5),
        'b_ada': nrm((DEPTH, 6 * d), 0.02),
        'w_in_even': nrm((N_EVEN, d, EVEN_IN), d ** -0.5),
        'q_norm': 1.0 + nrm((N_EVEN, Q_LORA), 0.02),
        'kv_norm': 1.0 + nrm((N_EVEN, KV_LORA), 0.02),
        'w_uq': nrm((N_EVEN, Q_LORA, MLA_HEADS * (QK_NOPE + QK_ROPE)), Q_LORA ** -0.5),
        'w_ukv': nrm((N_EVEN, KV_LORA, MLA_HEADS * (QK_NOPE + V_HEAD)), KV_LORA ** -0.5),
        'w_pool': nrm((N_EVEN, POOL_GROUPS, POOL_GC, POOL_GC), POOL_GC ** -0.5),
        'pool_scale': 1.0 + nrm((N_EVEN, POOL_CH), 0.1),
        'w_out_even': nrm((N_EVEN, EVEN_MIX, d), DN_BETA * EVEN_MIX ** -0.5),
        'w_in_odd': nrm((N_ODD, d, ODD_IN), d ** -0.5),
        'sink': nrm((N_ODD, SWA_HEADS)),
        'w_out_odd': nrm((N_ODD, ODD_MIX, d), DN_BETA * ODD_MIX ** -0.5),
        'ln1_g': 1.0 + nrm((DEPTH, d), 0.02),
        'ln1_b': nrm((DEPTH, d), 0.02),
        'w_router': nrm((DEPTH, d, N_EXPERTS), d ** -0.5),
        'w_gate': nrm((DEPTH, N_EXPERTS, d, EXPERT_FF), d ** -0.5),
        'w_up': nrm((DEPTH, N_EXPERTS, d, EXPERT_FF), d ** -0.5),
        'w_down': nrm((DEPTH, N_EXPERTS, EXPERT_FF, d), DN_BETA * EXPERT_FF ** -0.5),
        'ln2_g': 1.0 + nrm((DEPTH, d), 0.02),
        'ln2_b': nrm((DEPTH, d), 0.02),
    }


def reference(x_prompt, x_sample, cache_mla_ckv, cache_mla_kpe, cache_swa_k, cache_swa_v, c, c_ctx,
              w_ada, b_ada, w_in_even, q_norm, kv_norm, w_uq, w_ukv, w_pool, pool_scale, w_out_even,
              w_in_odd, sink, w_out_odd, ln1_g, ln1_b, w_router, w_gate, w_up, w_down, ln2_g, ln2_b):
    P = {
        'w_ada': w_ada, 'b_ada': b_ada,
        'w_in_even': w_in_even, 'q_norm': q_norm, 'kv_norm': kv_norm, 'w_uq': w_uq, 'w_ukv': w_ukv,
        'w_pool': w_pool, 'pool_scale': pool_scale, 'w_out_even': w_out_even,
        'w_in_odd': w_in_odd, 'sink': sink, 'w_out_odd': w_out_odd,
        'ln1_g': ln1_g, 'ln1_b': ln1_b, 'ln2_g': ln2_g, 'ln2_b': ln2_b,
        'w_router': w_router, 'w_gate': w_gate, 'w_up': w_up, 'w_down': w_down,
    }
    y_prompt, (ckv_l, kpe_l, k_l, v_l) = run_trunk(x_prompt, c_ctx[None, :], P, None)
    new_mla_ckv = jnp.stack(ckv_l, axis=1)
    new_mla_kpe = jnp.stack(kpe_l, axis=1)
    new_swa_k = jnp.stack(k_l, axis=1)
    new_swa_v = jnp.stack(v_l, axis=1)
    y_sample, _ = run_trunk(x_sample, c, P, (cache_mla_ckv, cache_mla_kpe, cache_swa_k, cache_swa_v))
    return (y_prompt, y_sample, new_mla_ckv, new_mla_kpe, new_swa_k, new_swa_v)
```

```python
import os
import numpy as np
import concourse.bass as bass
import concourse.mybir as mybir
from concourse.bass_utils import run_bass_kernel_spmd

F32 = mybir.dt.float32
BF16 = mybir.dt.bfloat16
AF = mybir.ActivationFunctionType
ALU = mybir.AluOpType

D = 2048
NT = 12
NTOK = 1536
SEQS = [(0, 8), (8, 10), (10, 12)]
ALPHA = 4.0 ** 0.25
RMS_EPS = 1e-6
LN_EPS = 1e-5
ENGS = ("pe", "act", "dve", "pool", "sp")
EPOCH = 4000
NDMASEM = 8


class Op:
    __slots__ = ("eng", "fn", "deps", "is_dma", "qidx", "inc_no", "needs_inc", "gid")


class Prog:
    def __init__(self, nc):
        self.nc = nc
        self.ops = {e: [] for e in ENGS}
        self.rec = {}
        self.gid = 0
        self.ndma = {e: 0 for e in ENGS}
        self.dma_list = {e: [] for e in ENGS}
        self.since_barrier = []

    @staticmethod
    def region(ap):
        t = ap.tensor
        name = t.name
        off = int(ap.offset)
        dims = [(int(s), int(c)) for (s, c) in ap.ap]
        shape = [int(x) for x in t.shape]
        isdram = "DRam" in type(t).__name__
        if isdram:
            lo = off
            hi = off + sum(s * (c - 1) for s, c in dims if s > 0) + 1
            return (name, 0, 1, lo, hi)
        Fsz = 1
        for x in shape[1:]:
            Fsz *= x
        if "PSum" in type(t).__name__:
            return (name, 0, 128, 0, Fsz)
        p0 = off // Fsz
        f0 = off % Fsz
        pc = dims[0][1] if dims[0][0] == Fsz else 1
        rest = dims[1:] if dims[0][0] == Fsz else dims
        f1 = f0 + sum(s * (c - 1) for s, c in rest if s > 0) + 1
        return (name, p0, p0 + pc, f0, f1)

    def add(self, eng, fn, reads=(), writes=(), dma=False):
        op = Op()
        op.eng = eng
        op.fn = fn
        op.is_dma = dma
        op.needs_inc = False
        op.inc_no = None
        op.gid = self.gid
        self.gid += 1
        deps = set()
        rregs = [self.region(a) for a in reads]
        wregs = [self.region(a) for a in writes]
        for (name, p0, p1, f0, f1) in rregs:
            psum = name.startswith("ps") and name[2:].isdigit()
            for r in self.rec.get(name, ()):
                if r[0] < p1 and p0 < r[1] and r[2] < f1 and f0 < r[3]:
                    if r[5] or (psum and r[4].eng != eng):
                        deps.add(r[4])
        for (name, p0, p1, f0, f1) in wregs:
            lst = self.rec.get(name)
            if lst is None:
                continue
            keep = []
            for r in lst:
                if r[0] < p1 and p0 < r[1] and r[2] < f1 and f0 < r[3]:
                    deps.add(r[4])
                    if p0 <= r[0] and r[1] <= p1 and f0 <= r[2] and r[3] <= f1:
                        continue
                keep.append(r)
            self.rec[name] = keep
        for (name, p0, p1, f0, f1) in rregs:
            lst = self.rec.setdefault(name, [])
            key = (p0, p1, f0, f1)
            for r in lst:
                if (not r[5]) and r[4].eng == eng and (not r[4].is_dma) and (not dma) and (r[0], r[1], r[2], r[3]) == key:
                    r[4] = op
                    break
            else:
                lst.append([p0, p1, f0, f1, op, False])
        for (name, p0, p1, f0, f1) in wregs:
            self.rec.setdefault(name, []).append([p0, p1, f0, f1, op, True])
        deps.discard(op)
        fdeps = []
        for d in deps:
            if d.eng == "pe" and eng == "pe" and not d.is_dma and not dma:
                continue
            d.needs_inc = True
            fdeps.append(d)
        op.deps = fdeps
        if dma:
            op.qidx = self.ndma[eng]
            self.ndma[eng] += 1
            self.dma_list[eng].append(op)
        self.ops[eng].append(op)
        self.since_barrier.append(op)
        return op

    def barrier(self):
        last = {}
        dmas = []
        for o in self.since_barrier:
            if o.is_dma:
                dmas.append(o)
            else:
                last[o.eng] = o
        deps = list(last.values()) + dmas
        self.since_barrier = []
        self.rec = {}
        for e in ENGS:
            op = Op()
            op.eng = e
            op.fn = None
            op.is_dma = False
            op.needs_inc = False
            op.inc_no = None
            op.gid = self.gid
            self.gid += 1
            op.deps = [d for d in deps]
            for d in deps:
                d.needs_inc = True
            self.ops[e].append(op)

    def mm(self, out, lhsT, rhs, start, stop):
        self.add("pe", lambda e: e.matmul(out, lhsT, rhs, start=start, stop=stop, skip_group_check=True), [lhsT, rhs], [out])

    def tr(self, out, in_, ident):
        self.add("pe", lambda e: e.transpose(out, in_, ident), [in_, ident], [out])

    def dma(self, q, out, in_):
        self.add(q, lambda e: e.dma_start(out=out, in_=in_), [in_], [out], dma=True)

    def act(self, out, in_, func, bias=None, scale=None, accum_out=None):
        kw = {}
        reads = [in_]
        writes = [out]
        if bias is not None:
            kw["bias"] = bias
            if not isinstance(bias, (int, float)):
                reads.append(bias)
        if scale is not None:
            kw["scale"] = scale
            if not isinstance(scale, (int, float)):
                reads.append(scale)
        if accum_out is not None:
            kw["accum_out"] = accum_out
            writes.append(accum_out)
        self.add("act", lambda e: e.activation(out, in_, func, **kw), reads, writes)

    def ts(self, eng, out, in0, s1, s2, op0, op1=None, accum_out=None):
        reads = [in0]
        for s in (s1, s2):
            if s is not None and not isinstance(s, (int, float)):
                reads.append(s)
        writes = [out]
        kw = {}
        if op1 is not None:
            kw["op1"] = op1
        if accum_out is not None:
            kw["accum_out"] = accum_out
            writes.append(accum_out)
        self.add(eng, lambda e: e.tensor_scalar(out, in0, s1, s2, op0, **kw), reads, writes)

    def tt(self, eng, out, in0, in1, op):
        self.add(eng, lambda e: e.tensor_tensor(out, in0, in1, op), [in0, in1], [out])

    def stt(self, out, in0, scalar, in1, op0, op1):
        reads = [in0, in1]
        if not isinstance(scalar, (int, float)):
            reads.append(scalar)
        self.add("dve", lambda e: e.scalar_tensor_tensor(out, in0, scalar, in1, op0, op1), reads, [out])

    def copy(self, eng, out, in_):
        if eng == "act":
            self.add("act", lambda e: e.copy(out, in_), [in_], [out])
        else:
            self.add(eng, lambda e: e.tensor_copy(out, in_), [in_], [out])

    def memset(self, eng, out, val):
        self.add(eng, lambda e: e.memset(out, val), [], [out])

    def emit(self):
        nc = self.nc
        cnt = {e: 0 for e in ENGS}
        for e in ENGS:
            for op in self.ops[e]:
                if op.needs_inc and not op.is_dma:
                    op.inc_no = cnt[e]
                    cnt[e] += 1
        nsem = {e: max(1, (cnt[e] + EPOCH - 1) // EPOCH) for e in ENGS}
        esems = {e: [nc.alloc_semaphore(name=f"s_{e}_{i}") for i in range(nsem[e])] for e in ENGS}
        dsems = {e: [nc.alloc_semaphore(name=f"d_{e}_{i}") for i in range(NDMASEM)] for e in ENGS if self.ndma[e] > 0}
        engobj = {"pe": "tensor", "act": "scalar", "dve": "vector", "pool": "gpsimd", "sp": "sync"}

        def target(d):
            if d.is_dma:
                return (dsems[d.eng][d.qidx % NDMASEM], 16 * (d.qidx // NDMASEM + 1))
            return (esems[d.eng][d.inc_no // EPOCH], d.inc_no % EPOCH + 1)

        prog = self

        def run_engine(ename):
            def body(eng):
                waited = {}
                for op in prog.ops[ename]:
                    need = {}
                    for d in op.deps:
                        s, v = target(d)
                        if need.get(s.num, (None, 0))[1] < v:
                            need[s.num] = (s, v)
                    if op.is_dma and op.qidx >= NDMASEM:
                        prev = prog.dma_list[ename][op.qidx - NDMASEM]
                        s, v = target(prev)
                        if need.get(s.num, (None, 0))[1] < v:
                            need[s.num] = (s, v)
                    for k, (s, v) in need.items():
                        if waited.get(k, 0) < v:
                            eng.wait_ge(s, v)
                            waited[k] = v
                    if op.fn is None:
                        continue
                    ins = op.fn(eng)
                    if op.is_dma:
                        ins.then_inc(dsems[ename][op.qidx % NDMASEM], 16)
                    elif op.needs_inc:
                        ins.then_inc(esems[ename][op.inc_no // EPOCH], 1)
                for i, s in enumerate(dsems.get(ename, [])):
                    n = (prog.ndma[ename] - i + NDMASEM - 1) // NDMASEM
                    if n > 0 and waited.get(s.num, 0) < 16 * n:
                        eng.wait_ge(s, 16 * n)
            return body

        with nc.Block() as block:
            for ename in ENGS:
                if not self.ops[ename]:
                    continue
                getattr(block, engobj[ename])(run_engine(ename))


def build_program(stage=99, debug=False):
    nc = bass.Bass("TRN2", target_bir_lowering=False)
    P = Prog(nc)

    def din(name, shape, dt=F32):
        return nc.dram_tensor(name, list(shape), dt, kind="ExternalInput").ap()

    def dout(name, shape, dt=F32):
        return nc.dram_tensor(name, list(shape), dt, kind="ExternalOutput").ap()

    def dscr(name, shape, dt=F32):
        return nc.dram_tensor(name, list(shape), dt, kind="Internal").ap()

    x_in = din("x", [NTOK, D])
    condT = din("condT", [128, 16, 2])
    c_ckv = din("c_ckv", [512, 256])
    c_kpe = din("c_kpe", [512, 64])
    c_swk = din("c_swk", [512, 256])
    c_swv = din("c_swv", [512, 256])
    w_ada = din("w_ada", [2, D, 6 * D])
    b_ada = din("b_ada", [2, 6 * D])
    w_in_even = din("w_in_even", [D, 1856])
    q_norm = din("q_norm", [1, 512])
    kv_norm = din("kv_norm", [1, 256])
    w_uq = din("w_uq", [512, 1536])
    w_ukv = din("w_ukv", [256, 2048])
    w_pool = din("w_pool", [4, 256, 256])
    pool_scaleT = din("pool_scaleT", [128, 8])
    w_out_even = din("w_out_even", [D, D])
    w_in_odd = din("w_in_odd", [D, 2560])
    sink = din("sink", [1, 32])
    w_out_odd = din("w_out_odd", [D, D])
    ln1_g = din("ln1_g", [2, D])
    ln1_b = din("ln1_b", [2, D])
    ln2_g = din("ln2_g", [2, D])
    ln2_b = din("ln2_b", [2, D])
    w_router = din("w_router", [2, D, 16])
    if stage >= 4:
        w_gate = din("w_gate", [2, 16, D, D])
        w_up = din("w_up", [2, 16, D, D])
        w_down = din("w_down", [2, 16, D, D])
    c_ident = din("c_ident", [128, 128])
    c_ropeC = din("c_ropeC", [128, 1024])
    c_ropeS = din("c_ropeS", [128, 1024])
    c_swap = din("c_swap", [128, 128])
    c_ropeCt = din("c_ropeCt", [128, 8, 32])
    c_ropeSt = din("c_ropeSt", [128, 8, 32])
    c_invcnt = din("c_invcnt", [4, 1584])
    c_triu = din("c_triu", [128, 128])
    c_ones = din("c_ones", [128, 128])
    c_iotaf = din("c_iotaf", [128, 160])
    c_iotap = din("c_iotap", [128, 1])
    c_sel = din("c_sel", [16, 16, 128])
    c_mprev = din("c_mprev", [128, 128])
    c_mnext = din("c_mnext", [128, 128])

    y_out = dout("y", [NTOK, D])
    o_ckv = dout("o_ckv", [512, 256])
    o_kpe = dout("o_kpe", [512, 64])
    o_k = dout("o_k", [512, 256])
    o_v = dout("o_v", [512, 256])
    dbg = {}

    modrows = dscr("modrows", [2, 2, 6 * D])
    Xs = dscr("Xs", [NTOK, D])
    X1s = dscr("X1s", [NTOK, D])
    Ys = dscr("Ys", [16, 192, D], BF16)

    from contextlib import ExitStack
    phase = [ExitStack()]
    uniq = [0]

    def sb(name, shape, dt=F32):
        uniq[0] += 1
        return phase[0].enter_context(nc.sbuf_tensor(f"{name}_{uniq[0]}", list(shape), dt))

    def psb(name, shape, dt=F32):
        return nc.alloc_sbuf_tensor(name, list(shape), dt)

    def end_phase():
        P.barrier()
        phase[0].close()
        phase[0] = ExitStack()

    PS = [nc.alloc_psum_tensor(f"ps{i}", [128, 512], F32) for i in range(8)]

    ident = psb("ident", [128, 128])
    mcol = [psb(f"mcol{l}", [128, 96, 2]) for l in range(2)]
    P.dma("sp", ident[:], c_ident)

    class Rot:
        def __init__(self, items):
            self.items = items
            self.i = 0

        def get(self):
            r = self.items[self.i % len(self.items)]
            self.i += 1
            return r

    def load_panel(buf, wap, nchunks, ncols):
        P.dma("pool", buf[:, 0:nchunks, 0:ncols], wap.rearrange("(c p) f -> p c f", p=128))

    ct32 = sb("ct32", [128, 16, 2])
    scT = sb("scT", [128, 16, 2], BF16)
    P.dma("sp", ct32[:], condT)
    P.act(scT[:], ct32[:], AF.Silu)
    pan = Rot([sb(f"mpan{i}", [128, 16, 512], BF16) for i in range(3)])
    brow = Rot([sb(f"brow{i}", [2, 512]) for i in range(2)])
    mrow = Rot([sb(f"mrow{i}", [2, 512]) for i in range(2)])
    psr = Rot([PS[0], PS[1]])
    pst = Rot([PS[2], PS[3]])
    for l in range(2):
        for pn in range(24):
            buf = pan.get()
            load_panel(buf, w_ada[l, :, pn * 512:(pn + 1) * 512], 16, 512)
            ps = psr.get()
            for c in range(16):
                P.mm(ps[0:2, :], scT[:, c, :], buf[:, c, :], c == 0, c == 15)
            br = brow.get()
            P.dma("sp", br[:], b_ada[l:l + 1, pn * 512:(pn + 1) * 512].partition_broadcast(2))
            mr = mrow.get()
            P.tt("dve", mr[:], ps[0:2, :], br[:], ALU.add)
            P.dma("sp", modrows[l, :, pn * 512:(pn + 1) * 512], mr[:])
            pt = pst.get()
            for k in range(4):
                P.tr(pt[:, k * 2:(k + 1) * 2], mr[0:2, k * 128:(k + 1) * 128], ident[0:2, 0:2])
            P.copy("dve", mcol[l][:, pn * 4:(pn + 1) * 4, :], pt[:, 0:8].rearrange("p (k r) -> p k r", r=2))
        for j in (1, 4):
            P.ts("dve", mcol[l][:, j * 16:(j + 1) * 16, :], mcol[l][:, j * 16:(j + 1) * 16, :], 1.0, None, ALU.add)

    def mc(l, j, c, r):
        return mcol[l][:, j * 16 + c, r:r + 1]

    if debug:
        dbg["d_mcol"] = dout("d_mcol", [2, 128, 96, 2])
        for l in range(2):
            P.dma("sp", dbg["d_mcol"][l], mcol[l][:])
        dbg["d_mod"] = dout("d_mod", [2, 2, 6 * D])
        dm = sb("dbg_m", [2, 2048])
        for l in range(2):
            for j in range(6):
                P.dma("sp", dm[:], modrows[l, :, j * D:(j + 1) * D])
                P.dma("sp", dbg["d_mod"][l, :, j * D:(j + 1) * D], dm[:])
    end_phase()
    if stage <= 0:
        P.emit()
        return nc, dbg

    evac_i = [0]

    def evac_affine(out, in_, scale, bias):
        import os
        mode = os.environ.get("KAFF", "both")
        evac_i[0] += 1
        if mode == "copy":
            P.copy("dve", out, in_)
        elif (evac_i[0] % 2 == 0 and mode == "both") or mode == "dve":
            P.ts("dve", out, in_, scale, bias, ALU.mult, ALU.add)
        else:
            P.act(out, in_, AF.Identity, bias=bias, scale=scale)

    def evac_copy(out, in_):
        evac_i[0] += 1
        if evac_i[0] % 2 == 0:
            P.copy("dve", out, in_)
        else:
            P.copy("act", out, in_)

    def prologue_hT(l, src, hT, xbufs, psrot):
        for t in range(NT):
            r = 0 if t < 8 else 1
            xb = xbufs.get()
            P.dma("sp", xb[:], src[t * 128:(t + 1) * 128, :])
            for cg in range(4):
                ps = psrot.get()
                for k in range(4):
                    c = cg * 4 + k
                    P.tr(ps[:, k * 128:(k + 1) * 128], xb[:, c * 128:(c + 1) * 128], ident[:])
                for k in range(4):
                    c = cg * 4 + k
                    evac_affine(hT[:, c, t * 128:(t + 1) * 128], ps[:, k * 128:(k + 1) * 128],
                                mc(l, 1, c, r), mc(l, 0, c, r))

    def bcast_load(dst, row_ap):
        P.dma("sp", dst, row_ap.partition_broadcast(128))

    def ln_tile(z, gbc, bbc, out, stats, mv, sm):
        for k in range(4):
            P.add("dve", (lambda e, k=k: e.bn_stats(stats[:, k, :], z[:, k * 512:(k + 1) * 512])),
                  [z[:, k * 512:(k + 1) * 512]], [stats[:, k, :]])
        P.add("dve", lambda e: e.bn_aggr(mv[:], stats[:].rearrange("p a b -> p (a b)")), [stats[:]], [mv[:]])
        P.act(sm[:, 0:1], mv[:, 1:2], AF.Sqrt, bias=LN_EPS, scale=1.0)
        P.add("dve", lambda e: e.reciprocal(sm[:, 1:2], sm[:, 0:1]), [sm[:, 0:1]], [sm[:, 1:2]])
        P.stt(sm[:, 2:3], mv[:, 0:1], -1.0, sm[:, 1:2], ALU.mult, ALU.mult)
        P.act(z, z, AF.Identity, bias=sm[:, 2:3], scale=sm[:, 1:2])
        P.tt("dve", z, z, gbc, ALU.mult)
        P.tt("dve", out, z, bbc, ALU.add)

    def outproj_ln_router(l, w_out, cat, xsrc, aff, names):
        wo = [sb(f"wo{names}{i}", [128, 16, 512], BF16) for i in range(4)]
        for pn in range(4):
            load_panel(wo[pn], w_out[:, pn * 512:(pn + 1) * 512], 16, 512)
        g1bc = [sb(f"g1bc{names}{r}", [128, D]) for r in range(2)]
        for r in range(2):
            bcast_load(g1bc[r][:], modrows[l, r:r + 1, 2 * D:3 * D])
        lng = sb(f"lng{names}", [128, D])
        lnb = sb(f"lnb{names}", [128, D])
        bcast_load(lng[:], ln1_g[l:l + 1, :])
        bcast_load(lnb[:], ln1_b[l:l + 1, :])
        wr = sb(f"wr{names}", [128, 16, 16])
        P.dma("sp", wr[:], w_router[l].rearrange("(c p) e -> p c e", p=128))
        wrm = [sb(f"wrm{names}{r}", [128, 16, 16]) for r in range(2)]
        rb = [sb(f"rb{names}{r}", [128, 16]) for r in range(2)]
        reps = Rot([sb(f"rep{names}{i}", [128, 128]) for i in range(2)])
        lg = sb(f"lg{names}", [128, 16])
        for r in range(2):
            for c in range(16):
                P.ts("dve", wrm[r][:, c, :], wr[:, c, :], mc(l, 4, c, r), None, ALU.mult)
            ps = PS[7]
            for c in range(16):
                rp = reps.get()
                P.copy("dve", rp[:], mc(l, 3, c, r).to_broadcast([128, 128]))
                P.mm(ps[:, 0:16], rp[:], wr[:, c, :], c == 0, c == 15)
            P.copy("dve", rb[r][:], ps[:, 0:16])
        xb = Rot([sb(f"xo{names}{i}", [128, D]) for i in range(2)])
        zb = Rot([sb(f"zo{names}{i}", [128, D]) for i in range(2)])
        tmpb = Rot([sb(f"to{names}{i}", [128, 512]) for i in range(2)])
        x1T = Rot([sb(f"x1T{names}{i}", [128, 16, 128]) for i in range(2)])
        stats = sb(f"st{names}", [128, 4, 6])
        mv = sb(f"mv{names}", [128, 2])
        sm = sb(f"sm{names}", [128, 4])
        ex = sb(f"ex{names}", [128, 16])
        psr = Rot([PS[0], PS[1], PS[2]])
        pstr = Rot([PS[3], PS[4], PS[5]])
        for t in range(NT):
            r = 0 if t < 8 else 1
            x = xb.get()
            P.dma("sp", x[:], xsrc[t * 128:(t + 1) * 128, :])
            z = zb.get()
            for pn in range(4):
                ps = psr.get()
                for c in range(16):
                    P.mm(ps[:, :], cat(c, t), wo[pn][:, c, :], c == 0, c == 15)
                tmp = tmpb.get()
                P.tt("dve", tmp[:], ps[:, :], g1bc[r][:, pn * 512:(pn + 1) * 512], ALU.mult)
                P.stt(z[:, pn * 512:(pn + 1) * 512], x[:, pn * 512:(pn + 1) * 512], ALPHA, tmp[:], ALU.mult, ALU.add)
            x1 = z
            if not os.environ.get("KR_NOLN"):
                ln_tile(z[:], lng[:], lnb[:], x1[:], stats, mv, sm)
            P.dma("sp", X1s[t * 128:(t + 1) * 128, :], x1[:])
            if os.environ.get("KR_NOROUTER"):
                P.memset("dve", aff[:, t, :], 0.0625)
                continue
            xT = x1T.get()
            for cg in range(4):
                ps = pstr.get()
                for k in range(4):
                    c = cg * 4 + k
                    P.tr(ps[:, k * 128:(k + 1) * 128], x1[:, c * 128:(c + 1) * 128], ident[:])
                evac_copy(xT[:, cg * 4:(cg + 1) * 4, :], ps[:, :].rearrange("p (k t) -> p k t", k=4))
            ps = PS[6]
            for c in range(16):
                P.mm(ps[:, 0:16], xT[:, c, :], wrm[r][:, c, :], c == 0, c == 15)
            P.tt("dve", lg[:], ps[:, 0:16], rb[r][:], ALU.add)
            P.add("dve", lambda e: e.reduce_max(sm[:, 3:4], lg[:], mybir.AxisListType.X), [lg[:]], [sm[:, 3:4]])
            P.ts("dve", sm[:, 3:4], sm[:, 3:4], -1.0, None, ALU.mult)
            P.act(ex[:], lg[:], AF.Exp, bias=sm[:, 3:4], scale=1.0, accum_out=sm[:, 0:1])
            P.add("dve", lambda e: e.reciprocal(sm[:, 1:2], sm[:, 0:1]), [sm[:, 0:1]], [sm[:, 1:2]])
            P.ts("dve", aff[:, t, :], ex[:], sm[:, 1:2], None, ALU.mult)

    def moe(l, aff, dst, names):
        keep = ExitStack()

        def ksb(name, shape, dt=F32):
            uniq[0] += 1
            return keep.enter_context(nc.sbuf_tensor(f"{name}_{uniq[0]}", list(shape), dt))
        iotaf = ksb(f"iotaf{names}", [128, 160])
        iotap = ksb(f"iotap{names}", [128, 1])
        sel = ksb(f"sel{names}", [16, 16, 128])
        mask = ksb(f"mask{names}", [128, NT, 16])
        pos = ksb(f"pos{names}", [128, NT, 16])
        codeT = ksb(f"codeT{names}", [16, NTOK])
        affhl = ksb(f"affhl{names}", [128, NT, 16, 2], BF16)
        triu = sb(f"triu{names}", [128, 128], BF16)
        onesb = sb(f"onesb{names}", [128, 128], BF16)
        t32 = sb(f"t32{names}", [128, 128])
        P.dma("sp", iotaf[:], c_iotaf)
        P.dma("sp", iotap[:], c_iotap)
        P.dma("sp", sel[:], c_sel)
        P.dma("sp", t32[:], c_triu)
        P.copy("dve", triu[:], t32[:])
        P.memset("dve", onesb[:], 1.0)
        affT = sb(f"affT{names}", [16, NTOK])
        affW = sb(f"affW{names}", [16, NTOK])
        maskT = sb(f"maskT{names}", [16, NTOK])
        m8 = sb(f"m8{names}", [16, 8])
        maskb = sb(f"maskb{names}", [128, NT, 16], BF16)
        code = sb(f"code{names}", [128, NT, 16])
        afft = sb(f"afft{names}", [128, NT, 16])
        for tb in range(3):
            ps = PS[tb]
            for k in range(4):
                t = tb * 4 + k
                P.tr(ps[0:16, k * 128:(k + 1) * 128], aff[:, t, :], ident[:])
            P.copy("dve", affT[:, tb * 512:(tb + 1) * 512], ps[0:16, :])
            P.copy("act", affW[:, tb * 512:(tb + 1) * 512], ps[0:16, :])
        for (t0, t1) in SEQS:
            n = (t1 - t0) * 128
            cap = n // 8
            seg = affW[:, t0 * 128:t1 * 128]
            for it in range(cap // 8):
                P.add("dve", lambda e, seg=seg: e.max(m8[:], seg), [seg], [m8[:]])
                if it < cap // 8 - 1:
                    P.add("dve", lambda e, seg=seg: e.match_replace(seg, m8[:], seg, -1.0), [seg, m8[:]], [seg])
            P.ts("dve", maskT[:, t0 * 128:t1 * 128], affT[:, t0 * 128:t1 * 128], m8[:, 7:8], None, ALU.is_ge)
        for tb in range(3):
            ps = PS[3 + tb]
            for k in range(4):
                t = tb * 4 + k
                P.tr(ps[:, k * 16:(k + 1) * 16], maskT[:, t * 128:(t + 1) * 128], ident[0:16, 0:16])
            P.copy("dve", mask[:, tb * 4:(tb + 1) * 4, :], ps[:, 0:64].rearrange("p (k e) -> p k e", k=4))
            P.copy("act", maskb[:, tb * 4:(tb + 1) * 4, :], ps[:, 0:64].rearrange("p (k e) -> p k e", k=4))
        for (t0, t1) in SEQS:
            for t in range(t0, t1):
                ps = PS[6 + (t % 2)]
                for j in range(t0, t):
                    P.mm(ps[:, 0:16], onesb[:], maskb[:, j, :], j == t0, False)
                P.mm(ps[:, 0:16], triu[:], maskb[:, t, :], t == t0, True)
                P.copy("dve", pos[:, t, :], ps[:, 0:16])
        P.stt(code[:], pos[:], 1.0, mask[:], ALU.add, ALU.mult)
        P.ts("dve", code[:], code[:], -1.0, None, ALU.add)
        for tb in range(3):
            ps = PS[tb]
            for k in range(4):
                t = tb * 4 + k
                P.tr(ps[0:16, k * 128:(k + 1) * 128], code[:, t, :], ident[:])
            P.copy("dve", codeT[:, tb * 512:(tb + 1) * 512], ps[0:16, :])
        P.copy("dve", affhl[:, :, :, 0], aff[:])
        P.tt("dve", afft[:], aff[:], affhl[:, :, :, 0], ALU.subtract)
        P.copy("dve", affhl[:, :, :, 1], afft[:])

        end_phase()
        x1bf = sb(f"x1bf{names}", [128, NT, D], BF16)
        xld = Rot([sb(f"xld{names}{i}", [128, D]) for i in range(1)])
        for t in range(NT):
            xl = xld.get()
            P.dma("sp", xl[:], X1s[t * 128:(t + 1) * 128, :])
            evac_copy(x1bf[:, t, :], xl[:])
        Sb = Rot([sb(f"S{names}{i}", [128, NT, 128], BF16) for i in range(2)])
        xsT = Rot([sb(f"xsT{names}{i}", [128, 16, 192], BF16) for i in range(2)])
        hTb = Rot([sb(f"hTm{names}{i}", [128, 16, 192], BF16) for i in range(2)])
        sg = Rot([sb(f"sg{names}{i}", [128, 192]) for i in range(2)])
        wvs = Rot([sb(f"wvs{names}{i}", [128, 2]) for i in range(2)])
        wtmp = Rot([sb(f"wtmp{names}{i}", [128, 4]) for i in range(2)])
        ys = Rot([sb(f"ys{names}{i}", [128, D], BF16) for i in range(2)])
        yp = Rot([sb(f"yp{names}{i}", [64, D], BF16) for i in range(2)])
        panels = Rot([sb(f"pan{names}{i}", [128, 16, 512], BF16) for i in range(5)])
        for e in range(16):
            S = Sb.get()
            for t in range(NT):
                if t < 8:
                    P.ts("dve", S[:, t, 0:128], iotaf[:, 32:160], pos[:, t, e:e + 1], mask[:, t, e:e + 1], ALU.is_equal, ALU.mult)
                elif t < 10:
                    P.ts("dve", S[:, t, 0:64], iotaf[:, 32:96], pos[:, t, e:e + 1], mask[:, t, e:e + 1], ALU.is_equal, ALU.mult)
                else:
                    P.ts("dve", S[:, t, 0:64], iotaf[:, 0:64], pos[:, t, e:e + 1], mask[:, t, e:e + 1], ALU.is_equal, ALU.mult)
            wv = wvs.get()
            ps = PS[7]
            for t in range(8):
                P.mm(ps[:, 0:2], S[:, t, 0:128], affhl[:, t, e, :], t == 0, t == 7)
            for t in range(8, 12):
                P.mm(ps[0:64, 2:4], S[:, t, 0:64], affhl[:, t, e, :], False, t == 11)
            wt = wtmp.get()
            P.copy("dve", wt[:, 0:2], ps[:, 0:2])
            P.copy("dve", wt[0:64, 2:4], ps[0:64, 2:4])
            P.tt("dve", wv[:, 0:1], wt[:, 0:1], wt[:, 1:2], ALU.add)
            P.tt("dve", wv[0:64, 1:2], wt[0:64, 2:3], wt[0:64, 3:4], ALU.add)
            xs = xsT.get()
            for cg in range(8):
                ps = PS[cg % 2]
                psv = ps[:, 0:384].rearrange("p (k s) -> p k s", k=2)
                first = True
                for k in range(2):
                    c = cg * 2 + k
                    for t in range(8):
                        P.mm(psv[:, k, 0:128], x1bf[:, t, c * 128:(c + 1) * 128], S[:, t, 0:128], first, False)
                        first = False
                    for t in range(8, 12):
                        P.mm(psv[:, k, 128:192], x1bf[:, t, c * 128:(c + 1) * 128], S[:, t, 0:64], False, t == 11)
                for k in range(2):
                    c = cg * 2 + k
                    evac_affine(xs[:, c, 0:128], psv[:, k, 0:128], mc(l, 4, c, 0), mc(l, 3, c, 0))
                    evac_affine(xs[:, c, 128:192], psv[:, k, 128:192], mc(l, 4, c, 1), mc(l, 3, c, 1))
            hT = hTb.get()
            for pn in range(4):
                gb = panels.get()
                load_panel(gb, w_gate[l, e, :, pn * 512:(pn + 1) * 512], 16, 512)
                ub = panels.get()
                load_panel(ub, w_up[l, e, :, pn * 512:(pn + 1) * 512], 16, 512)
                for f in range(4):
                    fc = pn * 4 + f
                    pg = PS[2 + (f % 2)]
                    pu = PS[4 + (f % 2)]
                    for c in range(16):
                        P.mm(pg[:, 0:192], gb[:, c, f * 128:(f + 1) * 128], xs[:, c, :], c == 0, c == 15)
                    for c in range(16):
                        P.mm(pu[:, 0:192], ub[:, c, f * 128:(f + 1) * 128], xs[:, c, :], c == 0, c == 15)
                    s = sg.get()
                    P.act(s[:], pg[:, 0:192], AF.Silu)
                    P.tt("dve", hT[:, fc, :], pu[:, 0:192], s[:], ALU.mult)
            y_s = ys.get()
            y_p = yp.get()
            for pn in range(4):
                db = panels.get()
                load_panel(db, w_down[l, e, :, pn * 512:(pn + 1) * 512], 16, 512)
                p1 = PS[6]
                p2 = PS[7]
                for fc in range(16):
                    P.mm(p1[:, :], hT[:, fc, 0:128], db[:, fc, :], fc == 0, fc == 15)
                for fc in range(16):
                    P.mm(p2[0:64, :], hT[:, fc, 128:192], db[:, fc, :], fc == 0, fc == 15)
                P.act(y_s[:, pn * 512:(pn + 1) * 512], p1[:, :], AF.Identity, scale=wv[:, 0:1])
                P.ts("dve", y_p[:, pn * 512:(pn + 1) * 512], p2[0:64, :], wv[0:64, 1:2], None, ALU.mult)
            P.dma("sp", Ys[e, 0:128, :], y_s[:])
            P.dma("sp", Ys[e, 128:192, :], y_p[:])
        end_phase()

        Yb = sb(f"Yb{names}", [128, 16, D], BF16)
        ST = sb(f"ST{names}", [128, 16, 1024], BF16)
        g2bc = [sb(f"g2bc{names}{r}", [128, D]) for r in range(2)]
        for r in range(2):
            bcast_load(g2bc[r][:], modrows[l, r:r + 1, 5 * D:6 * D])
        lng = sb(f"lng2{names}", [128, D])
        lnb = sb(f"lnb2{names}", [128, D])
        bcast_load(lng[:], ln2_g[l:l + 1, :])
        bcast_load(lnb[:], ln2_b[l:l + 1, :])
        xb = Rot([sb(f"xm{names}{i}", [128, D]) for i in range(2)])
        zb = Rot([sb(f"zm{names}{i}", [128, D]) for i in range(2)])
        tmpb = Rot([sb(f"tm{names}{i}", [128, 512]) for i in range(2)])
        stats = sb(f"st2{names}", [128, 4, 6])
        mv = sb(f"mv2{names}", [128, 2])
        sm = sb(f"sm2{names}", [128, 4])
        psr = Rot([PS[0], PS[1], PS[2]])
        psb = Rot([PS[3], PS[4]])
        for si, (t0, t1) in enumerate(SEQS):
            n = (t1 - t0) * 128
            K = 128 if si == 0 else 32
            s0 = 0 if si == 0 else (128 if si == 1 else 160)
            cofs = 0 if si < 2 else 0
            P.dma("sp", Yb[0:K, :, :], Ys[:, s0:s0 + K, :].rearrange("e s d -> s e d"))
            for e in range(16):
                for nb in range(0, n, 512):
                    w = min(512, n - nb)
                    ps = psb.get()
                    P.mm(ps[:, 0:w], sel[:, e, :], codeT[:, t0 * 128 + nb:t0 * 128 + nb + w], True, True)
                    P.ts("dve", ST[0:K, e, nb:nb + w], ps[0:K, 0:w], iotap[0:K, 0:1], None, ALU.is_equal)
            r = 0 if si == 0 else 1
            for t in range(t0, t1):
                x = xb.get()
                P.dma("sp", x[:], X1s[t * 128:(t + 1) * 128, :])
                z = zb.get()
                tl = (t - t0) * 128
                for pn in range(4):
                    ps = psr.get()
                    for e in range(16):
                        P.mm(ps[:, :], ST[0:K, e, tl:tl + 128], Yb[0:K, e, pn * 512:(pn + 1) * 512], e == 0, e == 15)
                    tmp = tmpb.get()
                    P.tt("dve", tmp[:], ps[:, :], g2bc[r][:, pn * 512:(pn + 1) * 512], ALU.mult)
                    P.stt(z[:, pn * 512:(pn + 1) * 512], x[:, pn * 512:(pn + 1) * 512], ALPHA, tmp[:], ALU.mult, ALU.add)
                ln_tile(z[:], lng[:], lnb[:], z[:], stats, mv, sm)
                P.dma("sp", dst[t * 128:(t + 1) * 128, :], z[:])
        end_phase()
        keep.close()

    aff = psb("aff", [128, NT, 16])
    mixer = [ExitStack()]

    def msb(name, shape, dt=F32):
        uniq[0] += 1
        return mixer[0].enter_context(nc.sbuf_tensor(f"{name}_{uniq[0]}", list(shape), dt))

    def dbg_dump(name, ap_sb, shape, dt=F32):
        dbg[name] = dout(name, shape, dt)
        P.dma("sp", dbg[name], ap_sb)

    SC_MLA = 192.0 ** -0.5
    if True:
        poolT = msb("poolT", [128, 8, NTOK], BF16)
        attnT = msb("attnT", [128, 8, NTOK], BF16)
        m2 = ExitStack()

        def m2sb(name, shape, dt=F32):
            uniq[0] += 1
            return m2.enter_context(nc.sbuf_tensor(f"{name}_{uniq[0]}", list(shape), dt))
        cqT = m2sb("cqT", [128, 4, NTOK], BF16)
        ckvT = m2sb("ckvT", [128, 2, 2048], BF16)
        kpeT = m2sb("kpeT", [64, 2048])
        hT = sb("hT0", [128, 16, NTOK], BF16)
        xbufs = Rot([sb(f"xa{i}", [128, D]) for i in range(1)])
        prologue_hT(0, x_in, hT, xbufs, Rot([PS[0], PS[1], PS[2], PS[3]]))
        import os
        if os.environ.get("KSUB", "z") == "a0":
            dbg_dump("d_hT", hT[:, :, 0:128], [128, 16, 128], BF16)
            end_phase()
            P.emit()
            return nc, dbg
        wA = sb("w_inA", [128, 16, 512], BF16)
        qnb = sb("qnb", [128, 512])
        kvnb = sb("kvnb", [128, 256])
        bcast_load(qnb[:], q_norm[0:1, :])
        bcast_load(kvnb[:], kv_norm[0:1, :])
        junk = sb("junk", [128, 512])
        cqn = Rot([sb(f"cqn{i}", [128, 512]) for i in range(1)])
        ckvn = Rot([sb(f"ckvn{i}", [128, 256]) for i in range(1)])
        kpe_t = Rot([sb(f"kpet{i}", [128, 64]) for i in range(2)])
        sm = sb("sm0", [128, 8])
        import os
        SUB = os.environ.get("KSUB", "z")

        def early(tag):
            if SUB == tag:
                if debug:
                    dbg_dump("d_hT", hT[:], [128, 16, NTOK], BF16)
                end_phase()
                P.emit()
                return True
            return False
        if early("a"):
            return nc, dbg
        load_panel(wA, w_in_even[:, 0:512], 16, 512)
        for t in range(NT):
            pa = PS[4 + (t % 2)]
            for c in range(16):
                P.mm(pa[:, :], hT[:, c, t * 128:(t + 1) * 128], wA[:, c, 0:512], c == 0, c == 15)
            P.act(junk[:, 0:512], pa[:, :], AF.Square, accum_out=sm[:, 0:1])
            P.act(sm[:, 2:3], sm[:, 0:1], AF.Sqrt, bias=RMS_EPS, scale=1.0 / 512)
            P.add("dve", lambda e: e.reciprocal(sm[:, 4:5], sm[:, 2:3]), [sm[:, 2:3]], [sm[:, 4:5]])
            cq = cqn.get()
            P.stt(cq[:], pa[:, :], sm[:, 4:5], qnb[:], ALU.mult, ALU.mult)
            pt = PS[6 + (t % 2)]
            for k in range(4):
                P.tr(pt[:, k * 128:(k + 1) * 128], cq[:, k * 128:(k + 1) * 128], ident[:])
            evac_copy(cqT[:, :, t * 128:(t + 1) * 128], pt[:, :].rearrange("p (k t) -> p k t", k=4))
        if early("b"):
            return nc, dbg
        load_panel(wA, w_in_even[:, 512:832], 16, 320)
        for t in range(NT):
            pb = PS[4 + (t % 2)]
            for c in range(16):
                P.mm(pb[:, 0:320], hT[:, c, t * 128:(t + 1) * 128], wA[:, c, 0:320], c == 0, c == 15)
            P.act(junk[:, 0:256], pb[:, 0:256], AF.Square, accum_out=sm[:, 1:2])
            P.act(sm[:, 3:4], sm[:, 1:2], AF.Sqrt, bias=RMS_EPS, scale=1.0 / 256)
            P.add("dve", lambda e: e.reciprocal(sm[:, 5:6], sm[:, 3:4]), [sm[:, 3:4]], [sm[:, 5:6]])
            ck = ckvn.get()
            P.stt(ck[:], pb[:, 0:256], sm[:, 5:6], kvnb[:], ALU.mult, ALU.mult)
            kp = kpe_t.get()
            P.copy("act", kp[:], pb[:, 256:320])
            if t >= 8:
                P.dma("sp", o_ckv[(t - 8) * 128:(t - 7) * 128, :], ck[:])
                P.dma("sp", o_kpe[(t - 8) * 128:(t - 7) * 128, :], kp[:])
            pt2 = PS[6 + (t % 2)]
            for k in range(2):
                P.tr(pt2[:, k * 128:(k + 1) * 128], ck[:, k * 128:(k + 1) * 128], ident[:])
            P.tr(pt2[0:64, 256:384], kp[:], ident[:])
            evac_copy(ckvT[:, :, t * 128:(t + 1) * 128], pt2[:, 0:256].rearrange("p (k t) -> p k t", k=2))
            evac_copy(kpeT[:, t * 128:(t + 1) * 128], pt2[0:64, 256:384])
        if early("c"):
            return nc, dbg
        cbuf = Rot([sb(f"cb{i}", [128, 320]) for i in range(2)])
        for k4 in range(4):
            cb = cbuf.get()
            P.dma("sp", cb[:, 0:256], c_ckv[k4 * 128:(k4 + 1) * 128, :])
            P.dma("sp", cb[:, 256:320], c_kpe[k4 * 128:(k4 + 1) * 128, :])
            pt2 = PS[2 + (k4 % 2)]
            for k in range(2):
                P.tr(pt2[:, k * 128:(k + 1) * 128], cb[:, k * 128:(k + 1) * 128], ident[:])
            P.tr(pt2[0:64, 256:384], cb[:, 256:320], ident[:])
            evac_copy(ckvT[:, :, 1536 + k4 * 128:1536 + (k4 + 1) * 128], pt2[:, 0:256].rearrange("p (k t) -> p k t", k=2))
            evac_copy(kpeT[:, 1536 + k4 * 128:1536 + (k4 + 1) * 128], pt2[0:64, 256:384])
        if early("d"):
            return nc, dbg
        invc = sb("invc", [128, 1584])
        psc = sb("psc", [128, 8])
        P.dma("sp", psc[:], pool_scaleT)
        wp = sb("wp", [128, 4, 2, 256], BF16)
        for g in range(4):
            P.dma("pool", wp[:, g, :, :], w_pool[g].rearrange("(c p) f -> p c f", p=128))
        U = Rot([sb(f"U{i}", [128, 1584]) for i in range(2)])
        A1 = sb("A1", [128, 1584])
        A2 = sb("A2", [128, 1584])
        pooled = [sb(f"pooled{i}", [128, 1584], BF16) for i in range(2)]
        for u in U.items + [A1, A2]:
            P.memset("pool", u[:], 0.0)
        SEGC = [(8, 0, 512), (520, 512, 512), (1048, 1024, 256), (1320, 1280, 256)]
        for fc in range(8):
            g = fc // 2
            w = (2, 4, 8, 16)[g]
            if fc % 4 == 0:
                load_panel(wA, w_in_even[:, 832 + fc * 128:832 + (fc + 4) * 128], 16, 512)
            if fc % 2 == 0:
                bcast_load(invc[:], c_invcnt[g:g + 1, :])
            u = U.get()
            for tb in range(3):
                ps = PS[4 + (tb % 2)]
                for c in range(16):
                    P.mm(ps[:, :], wA[:, c, (fc % 4) * 128:(fc % 4 + 1) * 128], hT[:, c, tb * 512:(tb + 1) * 512], c == 0, c == 15)
                if tb < 2:
                    evac_copy(u[:, 8 + tb * 512:8 + (tb + 1) * 512], ps[:, :])
                else:
                    evac_copy(u[:, 1048:1304], ps[:, 0:256])
                    evac_copy(u[:, 1320:1576], ps[:, 256:512])
            L = 1584
            src = u
            step = 1
            bufs = [A1, A2]
            bi = 0
            while step < w:
                dstb = bufs[bi % 2]
                bi += 1
                P.tt("pool", dstb[:, 0:L - step], src[:, 0:L - step], src[:, step:L], ALU.add)
                src = dstb
                step *= 2
            h = w // 2
            other = bufs[bi % 2]
            P.tt("dve", other[:, 8:1576], src[:, 8 - h:1576 - h], invc[:, 8:1576], ALU.mult)
            pl = pooled[fc % 2]
            P.tt("dve", pl[:, 8:1576], other[:, 8:1576], u[:, 8:1576], ALU.subtract)
            if fc % 2 == 1:
                pl0 = pooled[0]
                for ec in range(2):
                    for (pc, tk, n) in SEGC:
                        ps = PS[6 + (ec % 2)]
                        P.mm(ps[:, 0:n], wp[:, g, 0, ec * 128:(ec + 1) * 128], pl0[:, pc:pc + n], True, False)
                        P.mm(ps[:, 0:n], wp[:, g, 1, ec * 128:(ec + 1) * 128], pl[:, pc:pc + n], False, True)
                        ch = 2 * g + ec
                        P.ts("dve", poolT[:, ch, tk:tk + n], ps[:, 0:n], psc[:, ch:ch + 1], None, ALU.mult)
        if debug:
            dbg_dump("d_hT", hT[:], [128, 16, NTOK], BF16)
            dbg_dump("d_ckvT", ckvT[:], [128, 2, 2048], BF16)
            dbg_dump("d_cqT", cqT[:], [128, 4, NTOK], BF16)
            dbg_dump("d_kpeT", kpeT[:], [64, 2048])
            dbg_dump("d_poolT", poolT[:], [128, 8, NTOK], BF16)
        end_phase()
        if stage <= 1:
            P.emit()
            return nc, dbg

        ropeC = sb("ropeC", [64, 1024])
        ropeS = sb("ropeS", [64, 1024])
        swp = sb("swp", [64, 64])
        P.dma("sp", ropeC[:], c_ropeC[0:64, :])
        P.dma("sp", ropeS[:], c_ropeS[0:64, :])
        P.dma("sp", swp[:], c_swap[0:64, 0:64])
        kpeK = sb("kpeK", [64, 2048], BF16)
        r32 = sb("r32", [64, 512])
        r32b = sb("r32b", [64, 512])

        def rope_fm(out_bf, in32, pos0, n, ps):
            P.mm(ps[0:64, 0:n], swp[:], in32, True, True)
            P.tt("dve", r32b[:, 0:n], ps[0:64, 0:n], ropeS[:, pos0:pos0 + n], ALU.mult)
            P.tt("pool", r32[:, 0:n], in32, ropeC[:, pos0:pos0 + n], ALU.mult)
            P.tt("dve", out_bf, r32[:, 0:n], r32b[:, 0:n], ALU.add)

        for nb in range(2):
            rope_fm(kpeK[:, nb * 512:(nb + 1) * 512], kpeT[:, nb * 512:(nb + 1) * 512], nb * 512, 512, PS[nb])
        P.copy("dve", kpeK[:, 1024:2048], kpeT[:, 1024:2048])
        wq_h = Rot([sb(f"wqh{i}", [128, 4, 192], BF16) for i in range(2)])
        wkv_h = Rot([sb(f"wkvh{i}", [128, 2, 256], BF16) for i in range(2)])
        qn = sb("qn_sb", [128, 1024], BF16)
        qr = sb("qr_sb", [64, 1024], BF16)
        q32 = sb("q32", [64, 512])
        kn = sb("kn_sb", [128, 1536], BF16)
        vv = sb("v_sb", [128, 12, 129], BF16)
        P.memset("dve", vv[:, :, 128:129], 1.0)
        pT = Rot([sb(f"pT{i}", [128, 512], BF16) for i in range(3)])
        ao = Rot([sb(f"ao{i}", [128, 128]) for i in range(3)])
        rd = sb("rd", [128, 4])
        for si, (t0, t1) in enumerate(SEQS):
            nq = (t1 - t0) * 128
            q0 = t0 * 128
            kcols = [(0, 1024), (1536, 512)] if si == 0 else [(q0, 256)]
            ktiles = []
            for (c0, n) in kcols:
                for k in range(n // 128):
                    ktiles.append(c0 + k * 128)
            nkt = len(ktiles)
            for h in range(8):
                wq = wq_h.get()
                wk = wkv_h.get()
                load_panel(wq, w_uq[:, h * 192:(h + 1) * 192], 4, 192)
                load_panel(wk, w_ukv[:, h * 256:(h + 1) * 256], 2, 256)
                for qb in range(0, nq, 512):
                    w = min(512, nq - qb)
                    ps = PS[0]
                    for c in range(4):
                        P.mm(ps[:, 0:w], wq[:, c, 0:128], cqT[:, c, q0 + qb:q0 + qb + w], c == 0, c == 3)
                    evac_copy(qn[:, qb:qb + w], ps[:, 0:w])
                    ps = PS[1]
                    for c in range(4):
                        P.mm(ps[0:64, 0:w], wq[:, c, 128:192], cqT[:, c, q0 + qb:q0 + qb + w], c == 0, c == 3)
                    if si == 0:
                        P.copy("act", q32[:, 0:w], ps[0:64, 0:w])
                        rope_fm(qr[:, qb:qb + w], q32[:, 0:w], qb, w, PS[2])
                    else:
                        evac_copy(qr[:, qb:qb + w], ps[0:64, 0:w])
                for ki, kc in enumerate(ktiles):
                    if ki % 4 == 0:
                        psk = PS[3 + ((ki // 4) % 2)]
                    for c in range(2):
                        P.mm(psk[:, (ki % 4) * 128:(ki % 4 + 1) * 128], wk[:, c, 0:128], ckvT[:, c, kc:kc + 128], c == 0, c == 1)
                    if ki % 4 == 3 or ki == nkt - 1:
                        k0 = (ki // 4) * 4
                        evac_copy(kn[:, k0 * 128:(ki + 1) * 128], psk[:, 0:(ki - k0 + 1) * 128])
                for ki, kc in enumerate(ktiles):
                    if ki % 4 == 0:
                        psv_ = PS[5 + ((ki // 4) % 2)]
                    for c in range(2):
                        P.mm(psv_[:, (ki % 4) * 128:(ki % 4 + 1) * 128], ckvT[:, c, kc:kc + 128], wk[:, c, 128:256], c == 0, c == 1)
                    if ki % 4 == 3 or ki == nkt - 1:
                        k0 = (ki // 4) * 4
                        evac_copy(vv[:, k0:ki + 1, 0:128], psv_[:, 0:(ki - k0 + 1) * 128].rearrange("p (k d) -> p k d", d=128))
                for qb in range(0, nq, 512):
                    w = min(512, nq - qb)
                    nqt = w // 128
                    accb = [PS[6], PS[7]]
                    for ki, kc in enumerate(ktiles):
                        pss = PS[ki % 3]
                        P.mm(pss[:, 0:w], kn[:, ki * 128:(ki + 1) * 128], qn[:, qb:qb + w], True, False)
                        P.mm(pss[:, 0:w], kpeK[:, kc:kc + 128], qr[:, qb:qb + w], False, True)
                        p = pT.get()
                        P.act(p[:, 0:w], pss[:, 0:w], AF.Exp, scale=SC_MLA)
                        for qt in range(nqt):
                            ab = accb[qt // 2]
                            o = (qt % 2) * 129
                            P.mm(ab[:, o:o + 129], p[:, qt * 128:(qt + 1) * 128], vv[:, ki, :],
                                 ki == 0 and qt % 2 == 0, ki == nkt - 1)
                    for qt in range(nqt):
                        ab = accb[qt // 2]
                        o = (qt % 2) * 129
                        P.add("dve", lambda e, ab=ab, o=o, qt=qt: e.reciprocal(rd[:, qt:qt + 1], ab[:, o + 128:o + 129]),
                              [ab[:, o + 128:o + 129]], [rd[:, qt:qt + 1]])
                        tg = t0 + qb // 128 + qt
                        a = ao.get()
                        P.ts("dve", a[:], ab[:, o:o + 128], rd[:, qt:qt + 1], None, ALU.mult)
                        pst_ = PS[3 + (qt % 2)]
                        P.tr(pst_[:, 0:128], a[:], ident[:])
                        evac_copy(attnT[:, h, tg * 128:(tg + 1) * 128], pst_[:, 0:128])
        if debug:
            dbg_dump("d_attnT", attnT[:], [128, 8, NTOK], BF16)
        end_phase()
        m2.close()
        if stage <= 2:
            P.emit()
            return nc, dbg

        def cat0(c, t):
            return attnT[:, c, t * 128:(t + 1) * 128] if c < 8 else poolT[:, c - 8, t * 128:(t + 1) * 128]
        outproj_ln_router(0, w_out_even, cat0, x_in, aff, "a")
        if debug:
            dbg_dump("d_aff", aff[:], [128, NT, 16])
        end_phase()
        mixer[0].close()
        mixer[0] = ExitStack()
        if debug:
            dbg["d_x1"] = dout("d_x1", [NTOK, D])
            xx = sb("dbgx", [128, D])
            for t in range(NT):
                P.dma("sp", xx[:], X1s[t * 128:(t + 1) * 128, :])
                P.dma("sp", dbg["d_x1"][t * 128:(t + 1) * 128, :], xx[:])
            end_phase()
        if stage <= 3:
            P.emit()
            return nc, dbg
        moe(0, aff, Xs, "a")
        if debug:
            dbg["d_x2"] = dout("d_x2", [NTOK, D])
            xx = sb("dbgx", [128, D])
            for t in range(NT):
                P.dma("sp", xx[:], Xs[t * 128:(t + 1) * 128, :])
                P.dma("sp", dbg["d_x2"][t * 128:(t + 1) * 128, :], xx[:])
            end_phase()
        if stage <= 4:
            P.emit()
            return nc, dbg

    if True:
        qT = msb("qT1", [128, 16, NTOK], BF16)
        kT2 = msb("kT2", [128, 4, 2048], BF16)
        vA = msb("vA", [128, 16, 4, 65], BF16)
        esink = msb("esink", [128, 32])
        hT = sb("hT1", [128, 16, NTOK], BF16)
        xbufs = Rot([sb(f"xb1{i}", [128, D]) for i in range(1)])
        prologue_hT(1, Xs, hT, xbufs, Rot([PS[0], PS[1], PS[2], PS[3]]))
        P.memset("dve", vA[:, :, :, 64:65], 1.0)
        bcast_load(esink[:], sink[0:1, :])
        P.act(esink[:], esink[:], AF.Exp)
        ropeC = sb("ropeC1", [128, 1024])
        ropeS = sb("ropeS1", [128, 1024])
        swp = sb("swp1", [128, 128])
        P.dma("sp", ropeC[:], c_ropeC)
        P.dma("sp", ropeS[:], c_ropeS)
        P.dma("sp", swp[:], c_swap)
        cCt = sb("cCt", [128, 8, 32])
        cSt = sb("cSt", [128, 8, 32])
        P.dma("sp", cCt[:], c_ropeCt)
        P.dma("sp", cSt[:], c_ropeSt)
        wpan = Rot([sb(f"wpan1{i}", [128, 16, 512], BF16) for i in range(2)])
        q32 = Rot([sb(f"q32b{i}", [128, 512]) for i in range(2)])
        r32 = sb("r32c", [128, 512])
        r32b = sb("r32d", [128, 512])
        for pn in range(4):
            wb = wpan.get()
            load_panel(wb, w_in_odd[:, pn * 512:(pn + 1) * 512], 16, 512)
            for f in range(4):
                ch = pn * 4 + f
                for tb in range(3):
                    ps = PS[(f * 3 + tb) % 3]
                    for c in range(16):
                        P.mm(ps[:, :], wb[:, c, f * 128:(f + 1) * 128], hT[:, c, tb * 512:(tb + 1) * 512], c == 0, c == 15)
                    if tb < 2:
                        q3 = q32.get()
                        P.copy("act", q3[:], ps[:, :])
                        ps2 = PS[3 + (tb % 2)]
                        P.mm(ps2[:, :], swp[:], q3[:], True, True)
                        P.tt("dve", r32b[:], ps2[:, :], ropeS[:, tb * 512:(tb + 1) * 512], ALU.mult)
                        P.tt("pool", r32[:], q3[:], ropeC[:, tb * 512:(tb + 1) * 512], ALU.mult)
                        P.tt("dve", qT[:, ch, tb * 512:(tb + 1) * 512], r32[:], r32b[:], ALU.add)
                    else:
                        evac_copy(qT[:, ch, 1024:1536], ps[:, :])
        wb = wpan.get()
        load_panel(wb, w_in_odd[:, 2048:2560], 16, 512)
        kvt = Rot([sb(f"kvt{i}", [128, 512]) for i in range(2)])
        kd = Rot([sb(f"kd{i}", [128, 4, 128]) for i in range(2)])
        kr = sb("kr", [128, 4, 64])
        tA = sb("tA", [128, 4, 32])
        tB = sb("tB", [128, 4, 32])

        def k_to_T(ksrc, col0, ps):
            kdd = kd.get()
            P.copy("pool", kdd[:, :, 0:64], ksrc)
            P.copy("pool", kdd[:, :, 64:128], ksrc)
            for g in range(4):
                P.tr(ps[:, g * 128:(g + 1) * 128], kdd[:, g, :], ident[:])
            evac_copy(kT2[:, :, col0:col0 + 128], ps[:, :].rearrange("p (g t) -> p g t", g=4))

        for t in range(NT):
            ps = PS[5 + (t % 2)]
            for c in range(16):
                P.mm(ps[:, :], hT[:, c, t * 128:(t + 1) * 128], wb[:, c, :], c == 0, c == 15)
            kv = kvt.get()
            evac_copy(kv[:], ps[:, :])
            if t >= 8:
                P.dma("sp", o_k[(t - 8) * 128:(t - 7) * 128, :], kv[:, 0:256])
                P.dma("sp", o_v[(t - 8) * 128:(t - 7) * 128, :], kv[:, 256:512])
            P.copy("dve", vA[:, t, :, 0:64], kv[:, 256:512].rearrange("p (g d) -> p g d", g=4))
            k4 = kv[:, 0:256].rearrange("p (g d) -> p g d", g=4)
            if t < 8:
                ke = kv[:, 0:256].rearrange("p (g i two) -> p g i two", g=4, two=2)[:, :, :, 0]
                ko = kv[:, 0:256].rearrange("p (g i two) -> p g i two", g=4, two=2)[:, :, :, 1]
                cb_ = cCt[:, t:t + 1, :].to_broadcast([128, 4, 32])
                sb_ = cSt[:, t:t + 1, :].to_broadcast([128, 4, 32])
                kre = kr[:].rearrange("p g (i two) -> p g i two", two=2)[:, :, :, 0]
                kro = kr[:].rearrange("p g (i two) -> p g i two", two=2)[:, :, :, 1]
                P.tt("dve", tA[:], ke, cb_, ALU.mult)
                P.tt("dve", tB[:], ko, sb_, ALU.mult)
                P.tt("dve", kre, tA[:], tB[:], ALU.subtract)
                P.tt("dve", tA[:], ke, sb_, ALU.mult)
                P.tt("dve", tB[:], ko, cb_, ALU.mult)
                P.tt("dve", kro, tA[:], tB[:], ALU.add)
                k_to_T(kr[:], t * 128, PS[3 + (t % 2)])
            else:
                k_to_T(k4, t * 128, PS[3 + (t % 2)])
        cbk = Rot([sb(f"cbk{i}", [128, 512]) for i in range(2)])
        for k4i in range(4):
            cb = cbk.get()
            P.dma("sp", cb[:, 0:256], c_swk[k4i * 128:(k4i + 1) * 128, :])
            P.dma("sp", cb[:, 256:512], c_swv[k4i * 128:(k4i + 1) * 128, :])
            P.copy("dve", vA[:, 12 + k4i, :, 0:64], cb[:, 256:512].rearrange("p (g d) -> p g d", g=4))
            k_to_T(cb[:, 0:256].rearrange("p (g d) -> p g d", g=4), 1536 + k4i * 128, PS[3 + (k4i % 2)])
        if debug:
            dbg_dump("d_qT1", qT[:], [128, 16, NTOK], BF16)
            dbg_dump("d_kT2", kT2[:], [128, 4, 2048], BF16)
        end_phase()
        if stage <= 5:
            P.emit()
            return nc, dbg

        catT = msb("catT1", [128, 16, NTOK], BF16)
        mprev = sb("mprev", [128, 128], BF16)
        mnext = sb("mnext", [128, 128], BF16)
        m32 = sb("m32", [128, 128])
        P.dma("sp", m32[:], c_mprev)
        P.copy("dve", mprev[:], m32[:])
        P.dma("sp", m32[:], c_mnext)
        P.copy("dve", mnext[:], m32[:])
        pTb = Rot([sb(f"pTo{i}", [128, 2, 4, 128], BF16) for i in range(3)])
        atok = Rot([sb(f"atok{i}", [128, 32, 64]) for i in range(2)])
        den = sb("den", [128, 8])
        SC_SWA = 64.0 ** -0.5
        for si, (t0, t1) in enumerate(SEQS):
            for t in range(t0, t1):
                at = atok.get()
                if si == 0:
                    kl = [(j, (mprev if j == t - 1 else (mnext if j == t + 1 else None))) for j in (t - 1, t, t + 1) if 0 <= j < 8]
                    kl += [(12 + j, None) for j in range(4)]
                else:
                    kl = [(j, None) for j in range(t0, t1)]
                for g in range(4):
                    acc = [PS[4 + (g % 2) * 2], PS[5 + (g % 2) * 2]]
                    for ki, (kt, msk) in enumerate(kl):
                        kcol = kt * 128 if kt < 12 else 1536 + (kt - 12) * 128
                        p = pTb.get()
                        for par in range(2):
                            pss = PS[(ki * 2 + par) % 4]
                            P.mm(pss[:, :], kT2[par * 64:(par + 1) * 64, g, kcol:kcol + 128],
                                 qT[par * 64:(par + 1) * 64, 4 * g:4 * g + 4, t * 128:(t + 1) * 128], True, True)
                            P.act(p[:, par, :, :], pss[:, :].rearrange("p (m q) -> p m q", m=4), AF.Exp, scale=SC_SWA)
                        if msk is not None:
                            pv = p[:].rearrange("p a m q -> p (a m) q")
                            P.tt("pool", pv, pv, msk[:, None, :].to_broadcast([128, 8, 128]), ALU.mult)
                        for hh in range(8):
                            m, par = hh // 2, hh % 2
                            ab = acc[hh // 4]
                            o = (hh % 4) * 65
                            P.mm(ab[:, o:o + 65], p[:, par, m, :], vA[:, kt, g, :], ki == 0 and hh % 4 == 0, ki == len(kl) - 1)
                    for half in range(2):
                        ab = acc[half]
                        abv = ab[:, 0:260].rearrange("p (h d) -> p h d", d=65)
                        P.tt("dve", den[:, half * 4:(half + 1) * 4], abv[:, :, 64], esink[:, g * 8 + half * 4:g * 8 + half * 4 + 4], ALU.add)
                        P.add("dve", lambda e, half=half: e.reciprocal(den[:, half * 4:(half + 1) * 4], den[:, half * 4:(half + 1) * 4]),
                              [den[:, half * 4:(half + 1) * 4]], [den[:, half * 4:(half + 1) * 4]])
                        P.tt("dve", at[:, g * 8 + half * 4:g * 8 + half * 4 + 4, :], abv[:, :, 0:64],
                             den[:, half * 4:(half + 1) * 4].unsqueeze(2).to_broadcast([128, 4, 64]), ALU.mult)
                atf = at[:].rearrange("p h d -> p (h d)")
                for cg in range(4):
                    ps = PS[cg % 4]
                    for k in range(4):
                        c = cg * 4 + k
                        P.tr(ps[:, k * 128:(k + 1) * 128], atf[:, c * 128:(c + 1) * 128], ident[:])
                    evac_copy(catT[:, cg * 4:(cg + 1) * 4, t * 128:(t + 1) * 128], ps[:, :].rearrange("p (k t) -> p k t", k=4))
        if debug:
            dbg_dump("d_catT1", catT[:], [128, 16, NTOK], BF16)
        end_phase()
        if stage <= 6:
            P.emit()
            return nc, dbg

        catS = dscr("catS", [128, 16, NTOK], BF16)
        P.dma("sp", catS, catT[:])
        end_phase()
        mixer[0].close()
        catT2 = sb("catT2", [128, 16, NTOK], BF16)
        P.dma("sp", catT2[:], catS)

        def cat1(c, t):
            return catT2[:, c, t * 128:(t + 1) * 128]
        outproj_ln_router(1, w_out_odd, cat1, Xs, aff, "b")
        end_phase()
        moe(1, aff, y_out, "b")
    P.emit()
    return nc, dbg


def _consts():
    f32 = np.float32
    c = {}
    c["c_ident"] = np.eye(128, dtype=f32)
    n_freq = 16
    inv = (10000.0 ** (-np.arange(n_freq, dtype=f32) / f32(n_freq))).astype(f32)
    tok = np.arange(1024)
    row = (tok // 64).astype(f32)
    col = (tok % 64).astype(f32)
    ang = np.concatenate([row[:, None] * inv[None, :], col[:, None] * inv[None, :]], -1).astype(f32)
    cos = np.cos(ang).astype(f32)
    sin = np.sin(ang).astype(f32)
    C = np.zeros((128, 1024), f32)
    S = np.zeros((128, 1024), f32)
    for f in range(128):
        i = (f % 64) // 2
        C[f] = cos[:, i]
        S[f] = -sin[:, i] if f % 2 == 0 else sin[:, i]
    c["c_ropeC"] = C
    c["c_ropeS"] = S
    sw = np.zeros((128, 128), f32)
    for f in range(0, 128, 2):
        sw[f + 1, f] = 1.0
        sw[f, f + 1] = 1.0
    c["c_swap"] = sw
    c["c_ropeCt"] = np.ascontiguousarray(cos.reshape(8, 128, 32).transpose(1, 0, 2))
    c["c_ropeSt"] = np.ascontiguousarray(sin.reshape(8, 128, 32).transpose(1, 0, 2))
    inv_c = np.ones((4, 1584), f32)
    for g, w in enumerate((2, 4, 8, 16)):
        for (pc, n) in ((8, 1024), (1048, 256), (1320, 256)):
            t = np.arange(n)
            lo = np.clip(t - w // 2, 0, n)
            hi = np.clip(t + w // 2, 0, n)
            inv_c[g, pc:pc + n] = (1.0 / (hi - lo).astype(f32)).astype(f32)
    c["c_invcnt"] = inv_c
    k = np.arange(128)
    c["c_triu"] = (k[:, None] < k[None, :]).astype(f32)
    c["c_ones"] = np.ones((128, 128), f32)
    c["c_iotaf"] = np.tile((np.arange(160) - 32).astype(f32)[None, :], (128, 1))
    c["c_iotap"] = np.arange(128, dtype=f32)[:, None].copy()
    sel = np.zeros((16, 16, 128), f32)
    for e in range(16):
        sel[e, e, :] = 1.0
    c["c_sel"] = sel
    c["c_mprev"] = (k[:, None] >= k[None, :]).astype(f32)
    c["c_mnext"] = (k[:, None] <= k[None, :]).astype(f32)
    return c


def _core_inputs(i, inp, consts):
    f = np.ascontiguousarray
    m = {}
    m["x"] = f(np.concatenate([inp["x_sample"][i], inp["x_prompt"][2 * i], inp["x_prompt"][2 * i + 1]], 0))
    cond = np.stack([inp["c"][i], inp["c_ctx"]], 0)
    m["condT"] = f(cond.reshape(2, 16, 128).transpose(2, 1, 0))
    m["c_ckv"] = f(inp["cache_mla_ckv"][i, 0])
    m["c_kpe"] = f(inp["cache_mla_kpe"][i, 0])
    m["c_swk"] = f(inp["cache_swa_k"][i, 0].reshape(512, 256))
    m["c_swv"] = f(inp["cache_swa_v"][i, 0].reshape(512, 256))
    return m


_SHARED = None


def _shared_inputs(inp):
    f = np.ascontiguousarray
    s = {}
    s["w_ada"] = inp["w_ada"]
    s["b_ada"] = inp["b_ada"]
    s["w_in_even"] = inp["w_in_even"][0]
    s["q_norm"] = inp["q_norm"]
    s["kv_norm"] = inp["kv_norm"]
    s["w_uq"] = inp["w_uq"][0]
    s["w_ukv"] = inp["w_ukv"][0]
    s["w_pool"] = inp["w_pool"][0]
    s["pool_scaleT"] = f(inp["pool_scale"][0].reshape(8, 128).T)
    s["w_out_even"] = inp["w_out_even"][0]
    s["w_in_odd"] = inp["w_in_odd"][0]
    s["sink"] = inp["sink"]
    s["w_out_odd"] = inp["w_out_odd"][0]
    for k in ("ln1_g", "ln1_b", "ln2_g", "ln2_b", "w_router", "w_gate", "w_up", "w_down"):
        s[k] = inp[k]
    return {k: f(np.asarray(v, dtype=np.float32)) for k, v in s.items()}


def _used_names(nc):
    return None


def kernel(**inputs):
    inp = {k: np.asarray(v) for k, v in inputs.items()}
    res = build_program()
    nc = res[0]
    consts = _consts()
    shared = _shared_inputs(inp)
    in_maps = []
    for i in range(8):
        m = dict(shared)
        m.update(consts)
        m.update(_core_inputs(i, inp, consts))
        in_maps.append(m)
    r = run_bass_kernel_spmd(nc, in_maps, core_ids=list(range(8)))
    outs = r.results
    y_prompt = np.zeros((16, 256, 2048), np.float32)
    y_sample = np.zeros((8, 1024, 2048), np.float32)
    ckv = np.zeros((16, 1, 256, 256), np.float32)
    kpe = np.zeros((16, 1, 256, 64), np.float32)
    sk = np.zeros((16, 1, 256, 4, 64), np.float32)
    sv = np.zeros((16, 1, 256, 4, 64), np.float32)
    for i in range(8):
        o = outs[i]
        y = np.asarray(o["y"])
        y_sample[i] = y[0:1024]
        y_prompt[2 * i] = y[1024:1280]
        y_prompt[2 * i + 1] = y[1280:1536]
        for j in range(2):
            ckv[2 * i + j, 0] = np.asarray(o["o_ckv"])[j * 256:(j + 1) * 256]
            kpe[2 * i + j, 0] = np.asarray(o["o_kpe"])[j * 256:(j + 1) * 256]
            sk[2 * i + j, 0] = np.asarray(o["o_k"])[j * 256:(j + 1) * 256].reshape(256, 4, 64)
            sv[2 * i + j, 0] = np.asarray(o["o_v"])[j * 256:(j + 1) * 256].reshape(256, 4, 64)
    return (y_prompt, y_sample, ckv, kpe, sk, sv)
```

```python
import os
import numpy as np
import concourse.bass as bass
import concourse.mybir as mybir
from concourse.bass_utils import run_bass_kernel_spmd

F32 = mybir.dt.float32
BF16 = mybir.dt.bfloat16
AF = mybir.ActivationFunctionType
ALU = mybir.AluOpType

D = 2048
NT = 12
NTOK = 1536
SEQS = [(0, 8), (8, 10), (10, 12)]
ALPHA = 4.0 ** 0.25
RMS_EPS = 1e-6
LN_EPS = 1e-5
ENGS = ("pe", "act", "dve", "pool", "sp")
EPOCH = 4000
NDMASEM = 8


class Op:
    __slots__ = ("eng", "fn", "deps", "is_dma", "qidx", "inc_no", "needs_inc", "gid")


class Prog:
    def __init__(self, nc):
        self.nc = nc
        self.ops = {e: [] for e in ENGS}
        self.rec = {}
        self.gid = 0
        self.ndma = {e: 0 for e in ENGS}
        self.dma_list = {e: [] for e in ENGS}
        self.since_barrier = []

    @staticmethod
    def region(ap):
        t = ap.tensor
        name = t.name
        off = int(ap.offset)
        dims = [(int(s), int(c)) for (s, c) in ap.ap]
        shape = [int(x) for x in t.shape]
        isdram = "DRam" in type(t).__name__
        if isdram:
            lo = off
            hi = off + sum(s * (c - 1) for s, c in dims if s > 0) + 1
            return (name, 0, 1, lo, hi)
        Fsz = 1
        for x in shape[1:]:
            Fsz *= x
        if "PSum" in type(t).__name__:
            return (name, 0, 128, 0, Fsz)
        p0 = off // Fsz
        f0 = off % Fsz
        pc = dims[0][1] if dims[0][0] == Fsz else 1
        rest = dims[1:] if dims[0][0] == Fsz else dims
        f1 = f0 + sum(s * (c - 1) for s, c in rest if s > 0) + 1
        return (name, p0, p0 + pc, f0, f1)

    def add(self, eng, fn, reads=(), writes=(), dma=False):
        op = Op()
        op.eng = eng
        op.fn = fn
        op.is_dma = dma
        op.needs_inc = False
        op.inc_no = None
        op.gid = self.gid
        self.gid += 1
        deps = set()
        rregs = [self.region(a) for a in reads]
        wregs = [self.region(a) for a in writes]
        for (name, p0, p1, f0, f1) in rregs:
            psum = name.startswith("ps") and name[2:].isdigit()
            for r in self.rec.get(name, ()):
                if r[0] < p1 and p0 < r[1] and r[2] < f1 and f0 < r[3]:
                    if r[5] or (psum and r[4].eng != eng):
                        deps.add(r[4])
        for (name, p0, p1, f0, f1) in wregs:
            lst = self.rec.get(name)
            if lst is None:
                continue
            keep = []
            for r in lst:
                if r[0] < p1 and p0 < r[1] and r[2] < f1 and f0 < r[3]:
                    deps.add(r[4])
                    if p0 <= r[0] and r[1] <= p1 and f0 <= r[2] and r[3] <= f1:
                        continue
                keep.append(r)
            self.rec[name] = keep
        for (name, p0, p1, f0, f1) in rregs:
            lst = self.rec.setdefault(name, [])
            key = (p0, p1, f0, f1)
            for r in lst:
                if (not r[5]) and r[4].eng == eng and (not r[4].is_dma) and (not dma) and (r[0], r[1], r[2], r[3]) == key:
                    r[4] = op
                    break
            else:
                lst.append([p0, p1, f0, f1, op, False])
        for (name, p0, p1, f0, f1) in wregs:
            self.rec.setdefault(name, []).append([p0, p1, f0, f1, op, True])
        deps.discard(op)
        fdeps = []
        for d in deps:
            if d.eng == "pe" and eng == "pe" and not d.is_dma and not dma:
                continue
            d.needs_inc = True
            fdeps.append(d)
        op.deps = fdeps
        if dma:
            op.qidx = self.ndma[eng]
            self.ndma[eng] += 1
            self.dma_list[eng].append(op)
        self.ops[eng].append(op)
        self.since_barrier.append(op)
        return op

    def barrier(self):
        last = {}
        dmas = []
        for o in self.since_barrier:
            if o.is_dma:
                dmas.append(o)
            else:
                last[o.eng] = o
        deps = list(last.values()) + dmas
        self.since_barrier = []
        self.rec = {}
        for e in ENGS:
            op = Op()
            op.eng = e
            op.fn = None
            op.is_dma = False
            op.needs_inc = False
            op.inc_no = None
            op.gid = self.gid
            self.gid += 1
            op.deps = [d for d in deps]
            for d in deps:
                d.needs_inc = True
            self.ops[e].append(op)

    def mm(self, out, lhsT, rhs, start, stop):
        self.add("pe", lambda e: e.matmul(out, lhsT, rhs, start=start, stop=stop, skip_group_check=True), [lhsT, rhs], [out])

    def tr(self, out, in_, ident):
        self.add("pe", lambda e: e.transpose(out, in_, ident), [in_, ident], [out])

    def dma(self, q, out, in_):
        self.add(q, lambda e: e.dma_start(out=out, in_=in_), [in_], [out], dma=True)

    def act(self, out, in_, func, bias=None, scale=None, accum_out=None):
        kw = {}
        reads = [in_]
        writes = [out]
        if bias is not None:
            kw["bias"] = bias
            if not isinstance(bias, (int, float)):
                reads.append(bias)
        if scale is not None:
            kw["scale"] = scale
            if not isinstance(scale, (int, float)):
                reads.append(scale)
        if accum_out is not None:
            kw["accum_out"] = accum_out
            writes.append(accum_out)
        self.add("act", lambda e: e.activation(out, in_, func, **kw), reads, writes)

    def ts(self, eng, out, in0, s1, s2, op0, op1=None, accum_out=None):
        reads = [in0]
        for s in (s1, s2):
            if s is not None and not isinstance(s, (int, float)):
                reads.append(s)
        writes = [out]
        kw = {}
        if op1 is not None:
            kw["op1"] = op1
        if accum_out is not None:
            kw["accum_out"] = accum_out
            writes.append(accum_out)
        self.add(eng, lambda e: e.tensor_scalar(out, in0, s1, s2, op0, **kw), reads, writes)

    def tt(self, eng, out, in0, in1, op):
        self.add(eng, lambda e: e.tensor_tensor(out, in0, in1, op), [in0, in1], [out])

    def stt(self, out, in0, scalar, in1, op0, op1):
        reads = [in0, in1]
        if not isinstance(scalar, (int, float)):
            reads.append(scalar)
        self.add("dve", lambda e: e.scalar_tensor_tensor(out, in0, scalar, in1, op0, op1), reads, [out])

    def copy(self, eng, out, in_):
        if eng == "act":
            self.add("act", lambda e: e.copy(out, in_), [in_], [out])
        else:
            self.add(eng, lambda e: e.tensor_copy(out, in_), [in_], [out])

    def memset(self, eng, out, val):
        self.add(eng, lambda e: e.memset(out, val), [], [out])

    def emit(self):
        nc = self.nc
        cnt = {e: 0 for e in ENGS}
        for e in ENGS:
            for op in self.ops[e]:
                if op.needs_inc and not op.is_dma:
                    op.inc_no = cnt[e]
                    cnt[e] += 1
        nsem = {e: max(1, (cnt[e] + EPOCH - 1) // EPOCH) for e in ENGS}
        esems = {e: [nc.alloc_semaphore(name=f"s_{e}_{i}") for i in range(nsem[e])] for e in ENGS}
        dsems = {e: [nc.alloc_semaphore(name=f"d_{e}_{i}") for i in range(NDMASEM)] for e in ENGS if self.ndma[e] > 0}
        engobj = {"pe": "tensor", "act": "scalar", "dve": "vector", "pool": "gpsimd", "sp": "sync"}

        def target(d):
            if d.is_dma:
                return (dsems[d.eng][d.qidx % NDMASEM], 16 * (d.qidx // NDMASEM + 1))
            return (esems[d.eng][d.inc_no // EPOCH], d.inc_no % EPOCH + 1)

        prog = self

        def run_engine(ename):
            def body(eng):
                waited = {}
                for op in prog.ops[ename]:
                    need = {}
                    for d in op.deps:
                        s, v = target(d)
                        if need.get(s.num, (None, 0))[1] < v:
                            need[s.num] = (s, v)
                    if op.is_dma and op.qidx >= NDMASEM:
                        prev = prog.dma_list[ename][op.qidx - NDMASEM]
                        s, v = target(prev)
                        if need.get(s.num, (None, 0))[1] < v:
                            need[s.num] = (s, v)
                    for k, (s, v) in need.items():
                        if waited.get(k, 0) < v:
                            eng.wait_ge(s, v)
                            waited[k] = v
                    if op.fn is None:
                        continue
                    ins = op.fn(eng)
                    if op.is_dma:
                        ins.then_inc(dsems[ename][op.qidx % NDMASEM], 16)
                    elif op.needs_inc:
                        ins.then_inc(esems[ename][op.inc_no // EPOCH], 1)
                for i, s in enumerate(dsems.get(ename, [])):
                    n = (prog.ndma[ename] - i + NDMASEM - 1) // NDMASEM
                    if n > 0 and waited.get(s.num, 0) < 16 * n:
                        eng.wait_ge(s, 16 * n)
            return body

        with nc.Block() as block:
            for ename in ENGS:
                if not self.ops[ename]:
                    continue
                getattr(block, engobj[ename])(run_engine(ename))


def build_program(stage=99, debug=False):
    nc = bass.Bass("TRN2", target_bir_lowering=False)
    P = Prog(nc)

    def din(name, shape, dt=F32):
        return nc.dram_tensor(name, list(shape), dt, kind="ExternalInput").ap()

    def dout(name, shape, dt=F32):
        return nc.dram_tensor(name, list(shape), dt, kind="ExternalOutput").ap()

    def dscr(name, shape, dt=F32):
        return nc.dram_tensor(name, list(shape), dt, kind="Internal").ap()

    x_in = din("x", [NTOK, D])
    condT = din("condT", [128, 16, 2])
    c_ckv = din("c_ckv", [512, 256])
    c_kpe = din("c_kpe", [512, 64])
    c_swk = din("c_swk", [512, 256])
    c_swv = din("c_swv", [512, 256])
    w_ada = din("w_ada", [2, D, 6 * D])
    b_ada = din("b_ada", [2, 6 * D])
    w_in_even = din("w_in_even", [D, 1856])
    q_norm = din("q_norm", [1, 512])
    kv_norm = din("kv_norm", [1, 256])
    w_uq = din("w_uq", [512, 1536])
    w_ukv = din("w_ukv", [256, 2048])
    w_pool = din("w_pool", [4, 256, 256])
    pool_scaleT = din("pool_scaleT", [128, 8])
    w_out_even = din("w_out_even", [D, D])
    w_in_odd = din("w_in_odd", [D, 2560])
    sink = din("sink", [1, 32])
    w_out_odd = din("w_out_odd", [D, D])
    ln1_g = din("ln1_g", [2, D])
    ln1_b = din("ln1_b", [2, D])
    ln2_g = din("ln2_g", [2, D])
    ln2_b = din("ln2_b", [2, D])
    w_router = din("w_router", [2, D, 16])
    if stage >= 4:
        w_gate = din("w_gate", [2, 16, D, D])
        w_up = din("w_up", [2, 16, D, D])
        w_down = din("w_down", [2, 16, D, D])
    c_ident = din("c_ident", [128, 128])
    c_ropeC = din("c_ropeC", [128, 1024])
    c_ropeS = din("c_ropeS", [128, 1024])
    c_swap = din("c_swap", [128, 128])
    c_ropeCt = din("c_ropeCt", [128, 8, 32])
    c_ropeSt = din("c_ropeSt", [128, 8, 32])
    c_invcnt = din("c_invcnt", [4, 1584])
    c_triu = din("c_triu", [128, 128])
    c_ones = din("c_ones", [128, 128])
    c_iotaf = din("c_iotaf", [128, 160])
    c_iotap = din("c_iotap", [128, 1])
    c_sel = din("c_sel", [16, 16, 128])
    c_mprev = din("c_mprev", [128, 128])
    c_mnext = din("c_mnext", [128, 128])

    y_out = dout("y", [NTOK, D])
    o_ckv = dout("o_ckv", [512, 256])
    o_kpe = dout("o_kpe", [512, 64])
    o_k = dout("o_k", [512, 256])
    o_v = dout("o_v", [512, 256])
    dbg = {}

    modrows = dscr("modrows", [2, 2, 6 * D])
    Xs = dscr("Xs", [NTOK, D])
    X1s = dscr("X1s", [NTOK, D])
    Ys = dscr("Ys", [16, 192, D], BF16)

    from contextlib import ExitStack
    phase = [ExitStack()]
    uniq = [0]

    def sb(name, shape, dt=F32):
        uniq[0] += 1
        return phase[0].enter_context(nc.sbuf_tensor(f"{name}_{uniq[0]}", list(shape), dt))

    def psb(name, shape, dt=F32):
        return nc.alloc_sbuf_tensor(name, list(shape), dt)

    def end_phase():
        P.barrier()
        phase[0].close()
        phase[0] = ExitStack()

    PS = [nc.alloc_psum_tensor(f"ps{i}", [128, 512], F32) for i in range(8)]

    ident = psb("ident", [128, 128])
    mcol = [psb(f"mcol{l}", [128, 96, 2]) for l in range(2)]
    P.dma("sp", ident[:], c_ident)

    class Rot:
        def __init__(self, items):
            self.items = items
            self.i = 0

        def get(self):
            r = self.items[self.i % len(self.items)]
            self.i += 1
            return r

    def load_panel(buf, wap, nchunks, ncols):
        P.dma("pool", buf[:, 0:nchunks, 0:ncols], wap.rearrange("(c p) f -> p c f", p=128))

    ct32 = sb("ct32", [128, 16, 2])
    scT = sb("scT", [128, 16, 2], BF16)
    P.dma("sp", ct32[:], condT)
    P.act(scT[:], ct32[:], AF.Silu)
    pan = Rot([sb(f"mpan{i}", [128, 16, 512], BF16) for i in range(3)])
    brow = Rot([sb(f"brow{i}", [2, 512]) for i in range(2)])
    mrow = Rot([sb(f"mrow{i}", [2, 512]) for i in range(2)])
    psr = Rot([PS[0], PS[1]])
    pst = Rot([PS[2], PS[3]])
    for l in range(2):
        for pn in range(24):
            buf = pan.get()
            load_panel(buf, w_ada[l, :, pn * 512:(pn + 1) * 512], 16, 512)
            ps = psr.get()
            for c in range(16):
                P.mm(ps[0:2, :], scT[:, c, :], buf[:, c, :], c == 0, c == 15)
            br = brow.get()
            P.dma("sp", br[:], b_ada[l:l + 1, pn * 512:(pn + 1) * 512].partition_broadcast(2))
            mr = mrow.get()
            P.tt("dve", mr[:], ps[0:2, :], br[:], ALU.add)
            P.dma("sp", modrows[l, :, pn * 512:(pn + 1) * 512], mr[:])
            pt = pst.get()
            for k in range(4):
                P.tr(pt[:, k * 2:(k + 1) * 2], mr[0:2, k * 128:(k + 1) * 128], ident[0:2, 0:2])
            P.copy("dve", mcol[l][:, pn * 4:(pn + 1) * 4, :], pt[:, 0:8].rearrange("p (k r) -> p k r", r=2))
        for j in (1, 4):
            P.ts("dve", mcol[l][:, j * 16:(j + 1) * 16, :], mcol[l][:, j * 16:(j + 1) * 16, :], 1.0, None, ALU.add)

    def mc(l, j, c, r):
        return mcol[l][:, j * 16 + c, r:r + 1]

    if debug:
        dbg["d_mcol"] = dout("d_mcol", [2, 128, 96, 2])
        for l in range(2):
            P.dma("sp", dbg["d_mcol"][l], mcol[l][:])
        dbg["d_mod"] = dout("d_mod", [2, 2, 6 * D])
        dm = sb("dbg_m", [2, 2048])
        for l in range(2):
            for j in range(6):
                P.dma("sp", dm[:], modrows[l, :, j * D:(j + 1) * D])
                P.dma("sp", dbg["d_mod"][l, :, j * D:(j + 1) * D], dm[:])
    end_phase()
    if stage <= 0:
        P.emit()
        return nc, dbg

    evac_i = [0]

    def evac_affine(out, in_, scale, bias):
        import os
        mode = os.environ.get("KAFF", "both")
        evac_i[0] += 1
        if mode == "copy":
            P.copy("dve", out, in_)
        elif (evac_i[0] % 2 == 0 and mode == "both") or mode == "dve":
            P.ts("dve", out, in_, scale, bias, ALU.mult, ALU.add)
        else:
            P.act(out, in_, AF.Identity, bias=bias, scale=scale)

    def evac_copy(out, in_):
        evac_i[0] += 1
        if evac_i[0] % 2 == 0:
            P.copy("dve", out, in_)
        else:
            P.copy("act", out, in_)

    def prologue_hT(l, src, hT, xbufs, psrot):
        for t in range(NT):
            r = 0 if t < 8 else 1
            xb = xbufs.get()
            P.dma("sp", xb[:], src[t * 128:(t + 1) * 128, :])
            for cg in range(4):
                ps = psrot.get()
                for k in range(4):
                    c = cg * 4 + k
                    P.tr(ps[:, k * 128:(k + 1) * 128], xb[:, c * 128:(c + 1) * 128], ident[:])
                for k in range(4):
                    c = cg * 4 + k
                    evac_affine(hT[:, c, t * 128:(t + 1) * 128], ps[:, k * 128:(k + 1) * 128],
                                mc(l, 1, c, r), mc(l, 0, c, r))

    def bcast_load(dst, row_ap):
        P.dma("sp", dst, row_ap.partition_broadcast(128))

    def ln_tile(z, gbc, bbc, out, stats, mv, sm):
        for k in range(4):
            P.add("dve", (lambda e, k=k: e.bn_stats(stats[:, k, :], z[:, k * 512:(k + 1) * 512])),
                  [z[:, k * 512:(k + 1) * 512]], [stats[:, k, :]])
        P.add("dve", lambda e: e.bn_aggr(mv[:], stats[:].rearrange("p a b -> p (a b)")), [stats[:]], [mv[:]])
        P.act(sm[:, 0:1], mv[:, 1:2], AF.Sqrt, bias=LN_EPS, scale=1.0)
        P.add("dve", lambda e: e.reciprocal(sm[:, 1:2], sm[:, 0:1]), [sm[:, 0:1]], [sm[:, 1:2]])
        P.stt(sm[:, 2:3], mv[:, 0:1], -1.0, sm[:, 1:2], ALU.mult, ALU.mult)
        P.act(z, z, AF.Identity, bias=sm[:, 2:3], scale=sm[:, 1:2])
        P.tt("dve", z, z, gbc, ALU.mult)
        P.tt("dve", out, z, bbc, ALU.add)

    def outproj_ln_router(l, w_out, cat, xsrc, aff, names):
        wo = [sb(f"wo{names}{i}", [128, 16, 512], BF16) for i in range(4)]
        for pn in range(4):
            load_panel(wo[pn], w_out[:, pn * 512:(pn + 1) * 512], 16, 512)
        g1bc = [sb(f"g1bc{names}{r}", [128, D]) for r in range(2)]
        for r in range(2):
            bcast_load(g1bc[r][:], modrows[l, r:r + 1, 2 * D:3 * D])
        lng = sb(f"lng{names}", [128, D])
        lnb = sb(f"lnb{names}", [128, D])
        bcast_load(lng[:], ln1_g[l:l + 1, :])
        bcast_load(lnb[:], ln1_b[l:l + 1, :])
        wr = sb(f"wr{names}", [128, 16, 16])
        P.dma("sp", wr[:], w_router[l].rearrange("(c p) e -> p c e", p=128))
        wrm = [sb(f"wrm{names}{r}", [128, 16, 16]) for r in range(2)]
        rb = [sb(f"rb{names}{r}", [128, 16]) for r in range(2)]
        reps = Rot([sb(f"rep{names}{i}", [128, 128]) for i in range(2)])
        lg = sb(f"lg{names}", [128, 16])
        for r in range(2):
            for c in range(16):
                P.ts("dve", wrm[r][:, c, :], wr[:, c, :], mc(l, 4, c, r), None, ALU.mult)
            ps = PS[7]
            for c in range(16):
                rp = reps.get()
                P.copy("dve", rp[:], mc(l, 3, c, r).to_broadcast([128, 128]))
                P.mm(ps[:, 0:16], rp[:], wr[:, c, :], c == 0, c == 15)
            P.copy("dve", rb[r][:], ps[:, 0:16])
        xb = Rot([sb(f"xo{names}{i}", [128, D]) for i in range(2)])
        zb = Rot([sb(f"zo{names}{i}", [128, D]) for i in range(2)])
        tmpb = Rot([sb(f"to{names}{i}", [128, 512]) for i in range(2)])
        x1T = Rot([sb(f"x1T{names}{i}", [128, 16, 128]) for i in range(2)])
        stats = sb(f"st{names}", [128, 4, 6])
        mv = sb(f"mv{names}", [128, 2])
        sm = sb(f"sm{names}", [128, 4])
        ex = sb(f"ex{names}", [128, 16])
        psr = Rot([PS[0], PS[1], PS[2]])
        pstr = Rot([PS[3], PS[4], PS[5]])
        def part_a(t):
            r = 0 if t < 8 else 1
            x = xb.get()
            P.dma("sp", x[:], xsrc[t * 128:(t + 1) * 128, :])
            z = zb.get()
            for pn in range(4):
                ps = psr.get()
                for c in range(16):
                    P.mm(ps[:, :], cat(c, t), wo[pn][:, c, :], c == 0, c == 15)
                tmp = tmpb.get()
                P.tt("dve", tmp[:], ps[:, :], g1bc[r][:, pn * 512:(pn + 1) * 512], ALU.mult)
                P.stt(z[:, pn * 512:(pn + 1) * 512], x[:, pn * 512:(pn + 1) * 512], ALPHA, tmp[:], ALU.mult, ALU.add)
            ln_tile(z[:], lng[:], lnb[:], z[:], stats, mv, sm)
            P.dma("sp", X1s[t * 128:(t + 1) * 128, :], z[:])
            return z

        def part_b(t, x1):
            r = 0 if t < 8 else 1
            xT = x1T.get()
            for cg in range(4):
                ps = pstr.get()
                for k in range(4):
                    c = cg * 4 + k
                    P.tr(ps[:, k * 128:(k + 1) * 128], x1[:, c * 128:(c + 1) * 128], ident[:])
                evac_copy(xT[:, cg * 4:(cg + 1) * 4, :], ps[:, :].rearrange("p (k t) -> p k t", k=4))
            ps = PS[6]
            for c in range(16):
                P.mm(ps[:, 0:16], xT[:, c, :], wrm[r][:, c, :], c == 0, c == 15)
            P.tt("dve", lg[:], ps[:, 0:16], rb[r][:], ALU.add)
            P.add("dve", lambda e: e.reduce_max(sm2[:, 3:4], lg[:], mybir.AxisListType.X), [lg[:]], [sm2[:, 3:4]])
            P.ts("dve", sm2[:, 3:4], sm2[:, 3:4], -1.0, None, ALU.mult)
            P.act(ex[:], lg[:], AF.Exp, bias=sm2[:, 3:4], scale=1.0, accum_out=sm2[:, 0:1])
            P.add("dve", lambda e: e.reciprocal(sm2[:, 1:2], sm2[:, 0:1]), [sm2[:, 0:1]], [sm2[:, 1:2]])
            P.ts("dve", aff[:, t, :], ex[:], sm2[:, 1:2], None, ALU.mult)

        sm2 = sb(f"smb{names}", [128, 4])
        prev = None
        for t in range(NT):
            zt = part_a(t)
            if prev is not None:
                part_b(*prev)
            prev = (t, zt)
        part_b(*prev)

    def moe(l, aff, dst, names):
        keep = ExitStack()

        def ksb(name, shape, dt=F32):
            uniq[0] += 1
            return keep.enter_context(nc.sbuf_tensor(f"{name}_{uniq[0]}", list(shape), dt))
        iotaf = ksb(f"iotaf{names}", [128, 160])
        iotap = ksb(f"iotap{names}", [128, 1])
        sel = ksb(f"sel{names}", [16, 16, 128])
        mask = ksb(f"mask{names}", [128, NT, 16])
        pos = ksb(f"pos{names}", [128, NT, 16])
        codeT = ksb(f"codeT{names}", [16, NTOK])
        affhl = ksb(f"affhl{names}", [128, NT, 16, 2], BF16)
        triu = sb(f"triu{names}", [128, 128], BF16)
        onesb = sb(f"onesb{names}", [128, 128], BF16)
        t32 = sb(f"t32{names}", [128, 128])
        P.dma("sp", iotaf[:], c_iotaf)
        P.dma("sp", iotap[:], c_iotap)
        P.dma("sp", sel[:], c_sel)
        P.dma("sp", t32[:], c_triu)
        P.copy("dve", triu[:], t32[:])
        P.memset("dve", onesb[:], 1.0)
        affT = sb(f"affT{names}", [16, NTOK])
        affW = sb(f"affW{names}", [16, NTOK])
        maskT = sb(f"maskT{names}", [16, NTOK])
        m8 = sb(f"m8{names}", [16, 8])
        maskb = sb(f"maskb{names}", [128, NT, 16], BF16)
        code = sb(f"code{names}", [128, NT, 16])
        afft = sb(f"afft{names}", [128, NT, 16])
        for tb in range(3):
            ps = PS[tb]
            for k in range(4):
                t = tb * 4 + k
                P.tr(ps[0:16, k * 128:(k + 1) * 128], aff[:, t, :], ident[:])
            P.copy("dve", affT[:, tb * 512:(tb + 1) * 512], ps[0:16, :])
            P.copy("act", affW[:, tb * 512:(tb + 1) * 512], ps[0:16, :])
        for (t0, t1) in SEQS:
            n = (t1 - t0) * 128
            cap = n // 8
            seg = affW[:, t0 * 128:t1 * 128]
            for it in range(cap // 8):
                P.add("dve", lambda e, seg=seg: e.max(m8[:], seg), [seg], [m8[:]])
                if it < cap // 8 - 1:
                    P.add("dve", lambda e, seg=seg: e.match_replace(seg, m8[:], seg, -1.0), [seg, m8[:]], [seg])
            P.ts("dve", maskT[:, t0 * 128:t1 * 128], affT[:, t0 * 128:t1 * 128], m8[:, 7:8], None, ALU.is_ge)
        for tb in range(3):
            ps = PS[3 + tb]
            for k in range(4):
                t = tb * 4 + k
                P.tr(ps[:, k * 16:(k + 1) * 16], maskT[:, t * 128:(t + 1) * 128], ident[0:16, 0:16])
            P.copy("dve", mask[:, tb * 4:(tb + 1) * 4, :], ps[:, 0:64].rearrange("p (k e) -> p k e", k=4))
            P.copy("act", maskb[:, tb * 4:(tb + 1) * 4, :], ps[:, 0:64].rearrange("p (k e) -> p k e", k=4))
        for (t0, t1) in SEQS:
            for t in range(t0, t1):
                ps = PS[6 + (t % 2)]
                for j in range(t0, t):
                    P.mm(ps[:, 0:16], onesb[:], maskb[:, j, :], j == t0, False)
                P.mm(ps[:, 0:16], triu[:], maskb[:, t, :], t == t0, True)
                P.copy("dve", pos[:, t, :], ps[:, 0:16])
        P.stt(code[:], pos[:], 1.0, mask[:], ALU.add, ALU.mult)
        P.ts("dve", code[:], code[:], -1.0, None, ALU.add)
        for tb in range(3):
            ps = PS[tb]
            for k in range(4):
                t = tb * 4 + k
                P.tr(ps[0:16, k * 128:(k + 1) * 128], code[:, t, :], ident[:])
            P.copy("dve", codeT[:, tb * 512:(tb + 1) * 512], ps[0:16, :])
        P.copy("dve", affhl[:, :, :, 0], aff[:])
        P.tt("dve", afft[:], aff[:], affhl[:, :, :, 0], ALU.subtract)
        P.copy("dve", affhl[:, :, :, 1], afft[:])

        end_phase()
        x1bf = sb(f"x1bf{names}", [128, NT, D], BF16)
        xld = Rot([sb(f"xld{names}{i}", [128, D]) for i in range(1)])
        for t in range(NT):
            xl = xld.get()
            P.dma("sp", xl[:], X1s[t * 128:(t + 1) * 128, :])
            evac_copy(x1bf[:, t, :], xl[:])
        Sb = Rot([sb(f"S{names}{i}", [128, NT, 128], BF16) for i in range(2)])
        xsT = Rot([sb(f"xsT{names}{i}", [128, 16, 192], BF16) for i in range(2)])
        hTb = Rot([sb(f"hTm{names}{i}", [128, 16, 192], BF16) for i in range(2)])
        sg = Rot([sb(f"sg{names}{i}", [128, 192]) for i in range(2)])
        wvs = Rot([sb(f"wvs{names}{i}", [128, 2]) for i in range(2)])
        wtmp = Rot([sb(f"wtmp{names}{i}", [128, 4]) for i in range(2)])
        ys = Rot([sb(f"ys{names}{i}", [128, D], BF16) for i in range(2)])
        yp = Rot([sb(f"yp{names}{i}", [64, D], BF16) for i in range(2)])
        panels = Rot([sb(f"pan{names}{i}", [128, 16, 512], BF16) for i in range(5)])
        for e in range(16):
            S = Sb.get()
            for t in range(NT):
                if t < 8:
                    P.ts("dve", S[:, t, 0:128], iotaf[:, 32:160], pos[:, t, e:e + 1], mask[:, t, e:e + 1], ALU.is_equal, ALU.mult)
                elif t < 10:
                    P.ts("dve", S[:, t, 0:64], iotaf[:, 32:96], pos[:, t, e:e + 1], mask[:, t, e:e + 1], ALU.is_equal, ALU.mult)
                else:
                    P.ts("dve", S[:, t, 0:64], iotaf[:, 0:64], pos[:, t, e:e + 1], mask[:, t, e:e + 1], ALU.is_equal, ALU.mult)
            wv = wvs.get()
            ps = PS[7]
            for t in range(8):
                P.mm(ps[:, 0:2], S[:, t, 0:128], affhl[:, t, e, :], t == 0, t == 7)
            for t in range(8, 12):
                P.mm(ps[0:64, 2:4], S[:, t, 0:64], affhl[:, t, e, :], False, t == 11)
            wt = wtmp.get()
            P.copy("dve", wt[:, 0:2], ps[:, 0:2])
            P.copy("dve", wt[0:64, 2:4], ps[0:64, 2:4])
            P.tt("dve", wv[:, 0:1], wt[:, 0:1], wt[:, 1:2], ALU.add)
            P.tt("dve", wv[0:64, 1:2], wt[0:64, 2:3], wt[0:64, 3:4], ALU.add)
            xs = xsT.get()
            for cg in range(8):
                ps = PS[cg % 2]
                psv = ps[:, 0:384].rearrange("p (k s) -> p k s", k=2)
                first = True
                for k in range(2):
                    c = cg * 2 + k
                    for t in range(8):
                        P.mm(psv[:, k, 0:128], x1bf[:, t, c * 128:(c + 1) * 128], S[:, t, 0:128], first, False)
                        first = False
                    for t in range(8, 12):
                        P.mm(psv[:, k, 128:192], x1bf[:, t, c * 128:(c + 1) * 128], S[:, t, 0:64], False, t == 11)
                for k in range(2):
                    c = cg * 2 + k
                    evac_affine(xs[:, c, 0:128], psv[:, k, 0:128], mc(l, 4, c, 0), mc(l, 3, c, 0))
                    evac_affine(xs[:, c, 128:192], psv[:, k, 128:192], mc(l, 4, c, 1), mc(l, 3, c, 1))
            hT = hTb.get()
            for pn in range(4):
                gb = panels.get()
                load_panel(gb, w_gate[l, e, :, pn * 512:(pn + 1) * 512], 16, 512)
                ub = panels.get()
                load_panel(ub, w_up[l, e, :, pn * 512:(pn + 1) * 512], 16, 512)
                for f in range(4):
                    fc = pn * 4 + f
                    pg = PS[2 + (f % 2)]
                    pu = PS[4 + (f % 2)]
                    for c in range(16):
                        P.mm(pg[:, 0:192], gb[:, c, f * 128:(f + 1) * 128], xs[:, c, :], c == 0, c == 15)
                    for c in range(16):
                        P.mm(pu[:, 0:192], ub[:, c, f * 128:(f + 1) * 128], xs[:, c, :], c == 0, c == 15)
                    s = sg.get()
                    P.act(s[:], pg[:, 0:192], AF.Silu)
                    P.tt("dve", hT[:, fc, :], pu[:, 0:192], s[:], ALU.mult)
            y_s = ys.get()
            y_p = yp.get()
            for pn in range(4):
                db = panels.get()
                load_panel(db, w_down[l, e, :, pn * 512:(pn + 1) * 512], 16, 512)
                p1 = PS[6]
                p2 = PS[7]
                for fc in range(16):
                    P.mm(p1[:, :], hT[:, fc, 0:128], db[:, fc, :], fc == 0, fc == 15)
                for fc in range(16):
                    P.mm(p2[0:64, :], hT[:, fc, 128:192], db[:, fc, :], fc == 0, fc == 15)
                P.act(y_s[:, pn * 512:(pn + 1) * 512], p1[:, :], AF.Identity, scale=wv[:, 0:1])
                P.ts("dve", y_p[:, pn * 512:(pn + 1) * 512], p2[0:64, :], wv[0:64, 1:2], None, ALU.mult)
            P.dma("sp", Ys[e, 0:128, :], y_s[:])
            P.dma("sp", Ys[e, 128:192, :], y_p[:])
        end_phase()

        Yb = sb(f"Yb{names}", [128, 16, D], BF16)
        ST = sb(f"ST{names}", [128, 16, 1024], BF16)
        g2bc = [sb(f"g2bc{names}{r}", [128, D]) for r in range(2)]
        for r in range(2):
            bcast_load(g2bc[r][:], modrows[l, r:r + 1, 5 * D:6 * D])
        lng = sb(f"lng2{names}", [128, D])
        lnb = sb(f"lnb2{names}", [128, D])
        bcast_load(lng[:], ln2_g[l:l + 1, :])
        bcast_load(lnb[:], ln2_b[l:l + 1, :])
        xb = Rot([sb(f"xm{names}{i}", [128, D]) for i in range(2)])
        zb = Rot([sb(f"zm{names}{i}", [128, D]) for i in range(2)])
        tmpb = Rot([sb(f"tm{names}{i}", [128, 512]) for i in range(2)])
        stats = sb(f"st2{names}", [128, 4, 6])
        mv = sb(f"mv2{names}", [128, 2])
        sm = sb(f"sm2{names}", [128, 4])
        psr = Rot([PS[0], PS[1], PS[2]])
        psb = Rot([PS[3], PS[4]])
        for si, (t0, t1) in enumerate(SEQS):
            n = (t1 - t0) * 128
            K = 128 if si == 0 else 32
            s0 = 0 if si == 0 else (128 if si == 1 else 160)
            cofs = 0 if si < 2 else 0
            P.dma("sp", Yb[0:K, :, :], Ys[:, s0:s0 + K, :].rearrange("e s d -> s e d"))
            for e in range(16):
                for nb in range(0, n, 512):
                    w = min(512, n - nb)
                    ps = psb.get()
                    P.mm(ps[:, 0:w], sel[:, e, :], codeT[:, t0 * 128 + nb:t0 * 128 + nb + w], True, True)
                    P.ts("dve", ST[0:K, e, nb:nb + w], ps[0:K, 0:w], iotap[0:K, 0:1], None, ALU.is_equal)
            r = 0 if si == 0 else 1
            for t in range(t0, t1):
                x = xb.get()
                P.dma("sp", x[:], X1s[t * 128:(t + 1) * 128, :])
                z = zb.get()
                tl = (t - t0) * 128
                for pn in range(4):
                    ps = psr.get()
                    for e in range(16):
                        P.mm(ps[:, :], ST[0:K, e, tl:tl + 128], Yb[0:K, e, pn * 512:(pn + 1) * 512], e == 0, e == 15)
                    tmp = tmpb.get()
                    P.tt("dve", tmp[:], ps[:, :], g2bc[r][:, pn * 512:(pn + 1) * 512], ALU.mult)
                    P.stt(z[:, pn * 512:(pn + 1) * 512], x[:, pn * 512:(pn + 1) * 512], ALPHA, tmp[:], ALU.mult, ALU.add)
                ln_tile(z[:], lng[:], lnb[:], z[:], stats, mv, sm)
                P.dma("sp", dst[t * 128:(t + 1) * 128, :], z[:])
        end_phase()
        keep.close()

    aff = psb("aff", [128, NT, 16])
    mixer = [ExitStack()]

    def msb(name, shape, dt=F32):
        uniq[0] += 1
        return mixer[0].enter_context(nc.sbuf_tensor(f"{name}_{uniq[0]}", list(shape), dt))

    def dbg_dump(name, ap_sb, shape, dt=F32):
        dbg[name] = dout(name, shape, dt)
        P.dma("sp", dbg[name], ap_sb)

    SC_MLA = 192.0 ** -0.5
    if True:
        poolT = msb("poolT", [128, 8, NTOK], BF16)
        attnT = msb("attnT", [128, 8, NTOK], BF16)
        m2 = ExitStack()

        def m2sb(name, shape, dt=F32):
            uniq[0] += 1
            return m2.enter_context(nc.sbuf_tensor(f"{name}_{uniq[0]}", list(shape), dt))
        cqT = m2sb("cqT", [128, 4, NTOK], BF16)
        ckvT = m2sb("ckvT", [128, 2, 2048], BF16)
        kpeT = m2sb("kpeT", [64, 2048])
        hT = sb("hT0", [128, 16, NTOK], BF16)
        xbufs = Rot([sb(f"xa{i}", [128, D]) for i in range(1)])
        prologue_hT(0, x_in, hT, xbufs, Rot([PS[0], PS[1], PS[2], PS[3]]))
        import os
        if os.environ.get("KSUB", "z") == "a0":
            dbg_dump("d_hT", hT[:, :, 0:128], [128, 16, 128], BF16)
            end_phase()
            P.emit()
            return nc, dbg
        wA = sb("w_inA", [128, 16, 512], BF16)
        qnb = sb("qnb", [128, 512])
        kvnb = sb("kvnb", [128, 256])
        bcast_load(qnb[:], q_norm[0:1, :])
        bcast_load(kvnb[:], kv_norm[0:1, :])
        junk = sb("junk", [128, 512])
        cqn = Rot([sb(f"cqn{i}", [128, 512]) for i in range(1)])
        ckvn = Rot([sb(f"ckvn{i}", [128, 256]) for i in range(1)])
        kpe_t = Rot([sb(f"kpet{i}", [128, 64]) for i in range(2)])
        sm = sb("sm0", [128, 8])
        import os
        SUB = os.environ.get("KSUB", "z")

        def early(tag):
            if SUB == tag:
                if debug:
                    dbg_dump("d_hT", hT[:], [128, 16, NTOK], BF16)
                end_phase()
                P.emit()
                return True
            return False
        if early("a"):
            return nc, dbg
        load_panel(wA, w_in_even[:, 0:512], 16, 512)
        for t in range(NT):
            pa = PS[4 + (t % 2)]
            for c in range(16):
                P.mm(pa[:, :], hT[:, c, t * 128:(t + 1) * 128], wA[:, c, 0:512], c == 0, c == 15)
            P.act(junk[:, 0:512], pa[:, :], AF.Square, accum_out=sm[:, 0:1])
            P.act(sm[:, 2:3], sm[:, 0:1], AF.Sqrt, bias=RMS_EPS, scale=1.0 / 512)
            P.add("dve", lambda e: e.reciprocal(sm[:, 4:5], sm[:, 2:3]), [sm[:, 2:3]], [sm[:, 4:5]])
            cq = cqn.get()
            P.stt(cq[:], pa[:, :], sm[:, 4:5], qnb[:], ALU.mult, ALU.mult)
            pt = PS[6 + (t % 2)]
            for k in range(4):
                P.tr(pt[:, k * 128:(k + 1) * 128], cq[:, k * 128:(k + 1) * 128], ident[:])
            evac_copy(cqT[:, :, t * 128:(t + 1) * 128], pt[:, :].rearrange("p (k t) -> p k t", k=4))
        if early("b"):
            return nc, dbg
        load_panel(wA, w_in_even[:, 512:832], 16, 320)
        for t in range(NT):
            pb = PS[4 + (t % 2)]
            for c in range(16):
                P.mm(pb[:, 0:320], hT[:, c, t * 128:(t + 1) * 128], wA[:, c, 0:320], c == 0, c == 15)
            P.act(junk[:, 0:256], pb[:, 0:256], AF.Square, accum_out=sm[:, 1:2])
            P.act(sm[:, 3:4], sm[:, 1:2], AF.Sqrt, bias=RMS_EPS, scale=1.0 / 256)
            P.add("dve", lambda e: e.reciprocal(sm[:, 5:6], sm[:, 3:4]), [sm[:, 3:4]], [sm[:, 5:6]])
            ck = ckvn.get()
            P.stt(ck[:], pb[:, 0:256], sm[:, 5:6], kvnb[:], ALU.mult, ALU.mult)
            kp = kpe_t.get()
            P.copy("act", kp[:], pb[:, 256:320])
            if t >= 8:
                P.dma("sp", o_ckv[(t - 8) * 128:(t - 7) * 128, :], ck[:])
                P.dma("sp", o_kpe[(t - 8) * 128:(t - 7) * 128, :], kp[:])
            pt2 = PS[6 + (t % 2)]
            for k in range(2):
                P.tr(pt2[:, k * 128:(k + 1) * 128], ck[:, k * 128:(k + 1) * 128], ident[:])
            P.tr(pt2[0:64, 256:384], kp[:], ident[:])
            evac_copy(ckvT[:, :, t * 128:(t + 1) * 128], pt2[:, 0:256].rearrange("p (k t) -> p k t", k=2))
            evac_copy(kpeT[:, t * 128:(t + 1) * 128], pt2[0:64, 256:384])
        if early("c"):
            return nc, dbg
        cbuf = Rot([sb(f"cb{i}", [128, 320]) for i in range(2)])
        for k4 in range(4):
            cb = cbuf.get()
            P.dma("sp", cb[:, 0:256], c_ckv[k4 * 128:(k4 + 1) * 128, :])
            P.dma("sp", cb[:, 256:320], c_kpe[k4 * 128:(k4 + 1) * 128, :])
            pt2 = PS[2 + (k4 % 2)]
            for k in range(2):
                P.tr(pt2[:, k * 128:(k + 1) * 128], cb[:, k * 128:(k + 1) * 128], ident[:])
            P.tr(pt2[0:64, 256:384], cb[:, 256:320], ident[:])
            evac_copy(ckvT[:, :, 1536 + k4 * 128:1536 + (k4 + 1) * 128], pt2[:, 0:256].rearrange("p (k t) -> p k t", k=2))
            evac_copy(kpeT[:, 1536 + k4 * 128:1536 + (k4 + 1) * 128], pt2[0:64, 256:384])
        if early("d"):
            return nc, dbg
        invc = sb("invc", [128, 1584])
        psc = sb("psc", [128, 8])
        P.dma("sp", psc[:], pool_scaleT)
        wp = sb("wp", [128, 4, 2, 256], BF16)
        for g in range(4):
            P.dma("pool", wp[:, g, :, :], w_pool[g].rearrange("(c p) f -> p c f", p=128))
        U = Rot([sb(f"U{i}", [128, 1584]) for i in range(2)])
        A1 = sb("A1", [128, 1584])
        A2 = sb("A2", [128, 1584])
        pooled = [sb(f"pooled{i}", [128, 1584], BF16) for i in range(2)]
        for u in U.items + [A1, A2]:
            P.memset("pool", u[:], 0.0)
        SEGC = [(8, 0, 512), (520, 512, 512), (1048, 1024, 256), (1320, 1280, 256)]
        for fc in range(8):
            g = fc // 2
            w = (2, 4, 8, 16)[g]
            if fc % 4 == 0:
                load_panel(wA, w_in_even[:, 832 + fc * 128:832 + (fc + 4) * 128], 16, 512)
            if fc % 2 == 0:
                bcast_load(invc[:], c_invcnt[g:g + 1, :])
            u = U.get()
            for tb in range(3):
                ps = PS[4 + (tb % 2)]
                for c in range(16):
                    P.mm(ps[:, :], wA[:, c, (fc % 4) * 128:(fc % 4 + 1) * 128], hT[:, c, tb * 512:(tb + 1) * 512], c == 0, c == 15)
                if tb < 2:
                    evac_copy(u[:, 8 + tb * 512:8 + (tb + 1) * 512], ps[:, :])
                else:
                    evac_copy(u[:, 1048:1304], ps[:, 0:256])
                    evac_copy(u[:, 1320:1576], ps[:, 256:512])
            L = 1584
            src = u
            step = 1
            bufs = [A1, A2]
            bi = 0
            while step < w:
                dstb = bufs[bi % 2]
                bi += 1
                P.tt("pool", dstb[:, 0:L - step], src[:, 0:L - step], src[:, step:L], ALU.add)
                src = dstb
                step *= 2
            h = w // 2
            other = bufs[bi % 2]
            P.tt("dve", other[:, 8:1576], src[:, 8 - h:1576 - h], invc[:, 8:1576], ALU.mult)
            pl = pooled[fc % 2]
            P.tt("dve", pl[:, 8:1576], other[:, 8:1576], u[:, 8:1576], ALU.subtract)
            if fc % 2 == 1:
                pl0 = pooled[0]
                for ec in range(2):
                    for (pc, tk, n) in SEGC:
                        ps = PS[6 + (ec % 2)]
                        P.mm(ps[:, 0:n], wp[:, g, 0, ec * 128:(ec + 1) * 128], pl0[:, pc:pc + n], True, False)
                        P.mm(ps[:, 0:n], wp[:, g, 1, ec * 128:(ec + 1) * 128], pl[:, pc:pc + n], False, True)
                        ch = 2 * g + ec
                        P.ts("dve", poolT[:, ch, tk:tk + n], ps[:, 0:n], psc[:, ch:ch + 1], None, ALU.mult)
        if debug:
            dbg_dump("d_hT", hT[:], [128, 16, NTOK], BF16)
            dbg_dump("d_ckvT", ckvT[:], [128, 2, 2048], BF16)
            dbg_dump("d_cqT", cqT[:], [128, 4, NTOK], BF16)
            dbg_dump("d_kpeT", kpeT[:], [64, 2048])
            dbg_dump("d_poolT", poolT[:], [128, 8, NTOK], BF16)
        end_phase()
        if stage <= 1:
            P.emit()
            return nc, dbg

        ropeC = sb("ropeC", [64, 1024])
        ropeS = sb("ropeS", [64, 1024])
        swp = sb("swp", [64, 64])
        P.dma("sp", ropeC[:], c_ropeC[0:64, :])
        P.dma("sp", ropeS[:], c_ropeS[0:64, :])
        P.dma("sp", swp[:], c_swap[0:64, 0:64])
        kpeK = sb("kpeK", [64, 2048], BF16)
        r32 = sb("r32", [64, 512])
        r32b = sb("r32b", [64, 512])

        def rope_fm(out_bf, in32, pos0, n, ps):
            P.mm(ps[0:64, 0:n], swp[:], in32, True, True)
            P.tt("dve", r32b[:, 0:n], ps[0:64, 0:n], ropeS[:, pos0:pos0 + n], ALU.mult)
            P.tt("pool", r32[:, 0:n], in32, ropeC[:, pos0:pos0 + n], ALU.mult)
            P.tt("dve", out_bf, r32[:, 0:n], r32b[:, 0:n], ALU.add)

        for nb in range(2):
            rope_fm(kpeK[:, nb * 512:(nb + 1) * 512], kpeT[:, nb * 512:(nb + 1) * 512], nb * 512, 512, PS[nb])
        P.copy("dve", kpeK[:, 1024:2048], kpeT[:, 1024:2048])
        wq_h = Rot([sb(f"wqh{i}", [128, 4, 192], BF16) for i in range(2)])
        wkv_h = Rot([sb(f"wkvh{i}", [128, 2, 256], BF16) for i in range(2)])
        qn = sb("qn_sb", [128, 1024], BF16)
        qr = sb("qr_sb", [64, 1024], BF16)
        q32 = sb("q32", [64, 512])
        kn = sb("kn_sb", [128, 1536], BF16)
        vv = sb("v_sb", [128, 12, 129], BF16)
        P.memset("dve", vv[:, :, 128:129], 1.0)
        pT = Rot([sb(f"pT{i}", [128, 512], BF16) for i in range(3)])
        ao = Rot([sb(f"ao{i}", [128, 128]) for i in range(3)])
        rd = sb("rd", [128, 4])
        for si, (t0, t1) in enumerate(SEQS):
            nq = (t1 - t0) * 128
            q0 = t0 * 128
            kcols = [(0, 1024), (1536, 512)] if si == 0 else [(q0, 256)]
            ktiles = []
            for (c0, n) in kcols:
                for k in range(n // 128):
                    ktiles.append(c0 + k * 128)
            nkt = len(ktiles)
            for h in range(8):
                wq = wq_h.get()
                wk = wkv_h.get()
                load_panel(wq, w_uq[:, h * 192:(h + 1) * 192], 4, 192)
                load_panel(wk, w_ukv[:, h * 256:(h + 1) * 256], 2, 256)
                for qb in range(0, nq, 512):
                    w = min(512, nq - qb)
                    ps = PS[0]
                    for c in range(4):
                        P.mm(ps[:, 0:w], wq[:, c, 0:128], cqT[:, c, q0 + qb:q0 + qb + w], c == 0, c == 3)
                    evac_copy(qn[:, qb:qb + w], ps[:, 0:w])
                    ps = PS[1]
                    for c in range(4):
                        P.mm(ps[0:64, 0:w], wq[:, c, 128:192], cqT[:, c, q0 + qb:q0 + qb + w], c == 0, c == 3)
                    if si == 0:
                        P.copy("act", q32[:, 0:w], ps[0:64, 0:w])
                        rope_fm(qr[:, qb:qb + w], q32[:, 0:w], qb, w, PS[2])
                    else:
                        evac_copy(qr[:, qb:qb + w], ps[0:64, 0:w])
                for ki, kc in enumerate(ktiles):
                    if ki % 4 == 0:
                        psk = PS[3 + ((ki // 4) % 2)]
                    for c in range(2):
                        P.mm(psk[:, (ki % 4) * 128:(ki % 4 + 1) * 128], wk[:, c, 0:128], ckvT[:, c, kc:kc + 128], c == 0, c == 1)
                    if ki % 4 == 3 or ki == nkt - 1:
                        k0 = (ki // 4) * 4
                        evac_copy(kn[:, k0 * 128:(ki + 1) * 128], psk[:, 0:(ki - k0 + 1) * 128])
                for ki, kc in enumerate(ktiles):
                    if ki % 4 == 0:
                        psv_ = PS[5 + ((ki // 4) % 2)]
                    for c in range(2):
                        P.mm(psv_[:, (ki % 4) * 128:(ki % 4 + 1) * 128], ckvT[:, c, kc:kc + 128], wk[:, c, 128:256], c == 0, c == 1)
                    if ki % 4 == 3 or ki == nkt - 1:
                        k0 = (ki // 4) * 4
                        evac_copy(vv[:, k0:ki + 1, 0:128], psv_[:, 0:(ki - k0 + 1) * 128].rearrange("p (k d) -> p k d", d=128))
                for qb in range(0, nq, 512):
                    w = min(512, nq - qb)
                    nqt = w // 128
                    accb = [PS[6], PS[7]]
                    def pv_mla(ki, p):
                        for qt in range(nqt):
                            ab = accb[qt // 2]
                            o = (qt % 2) * 129
                            P.mm(ab[:, o:o + 129], p[:, qt * 128:(qt + 1) * 128], vv[:, ki, :],
                                 ki == 0 and qt % 2 == 0, ki == nkt - 1)
                    pend = None
                    for ki, kc in enumerate(ktiles):
                        pss = PS[ki % 3]
                        P.mm(pss[:, 0:w], kn[:, ki * 128:(ki + 1) * 128], qn[:, qb:qb + w], True, False)
                        P.mm(pss[:, 0:w], kpeK[:, kc:kc + 128], qr[:, qb:qb + w], False, True)
                        p = pT.get()
                        P.act(p[:, 0:w], pss[:, 0:w], AF.Exp, scale=SC_MLA)
                        if pend is not None:
                            pv_mla(*pend)
                        pend = (ki, p)
                    pv_mla(*pend)
                    for qt in range(nqt):
                        ab = accb[qt // 2]
                        o = (qt % 2) * 129
                        P.add("dve", lambda e, ab=ab, o=o, qt=qt: e.reciprocal(rd[:, qt:qt + 1], ab[:, o + 128:o + 129]),
                              [ab[:, o + 128:o + 129]], [rd[:, qt:qt + 1]])
                        tg = t0 + qb // 128 + qt
                        a = ao.get()
                        P.ts("dve", a[:], ab[:, o:o + 128], rd[:, qt:qt + 1], None, ALU.mult)
                        pst_ = PS[3 + (qt % 2)]
                        P.tr(pst_[:, 0:128], a[:], ident[:])
                        evac_copy(attnT[:, h, tg * 128:(tg + 1) * 128], pst_[:, 0:128])
        if debug:
            dbg_dump("d_attnT", attnT[:], [128, 8, NTOK], BF16)
        end_phase()
        m2.close()
        if stage <= 2:
            P.emit()
            return nc, dbg

        def cat0(c, t):
            return attnT[:, c, t * 128:(t + 1) * 128] if c < 8 else poolT[:, c - 8, t * 128:(t + 1) * 128]
        outproj_ln_router(0, w_out_even, cat0, x_in, aff, "a")
        if debug:
            dbg_dump("d_aff", aff[:], [128, NT, 16])
        end_phase()
        mixer[0].close()
        mixer[0] = ExitStack()
        if debug:
            dbg["d_x1"] = dout("d_x1", [NTOK, D])
            xx = sb("dbgx", [128, D])
            for t in range(NT):
                P.dma("sp", xx[:], X1s[t * 128:(t + 1) * 128, :])
                P.dma("sp", dbg["d_x1"][t * 128:(t + 1) * 128, :], xx[:])
            end_phase()
        if stage <= 3:
            P.emit()
            return nc, dbg
        moe(0, aff, Xs, "a")
        if debug:
            dbg["d_x2"] = dout("d_x2", [NTOK, D])
            xx = sb("dbgx", [128, D])
            for t in range(NT):
                P.dma("sp", xx[:], Xs[t * 128:(t + 1) * 128, :])
                P.dma("sp", dbg["d_x2"][t * 128:(t + 1) * 128, :], xx[:])
            end_phase()
        if stage <= 4:
            P.emit()
            return nc, dbg

    if True:
        qT = msb("qT1", [128, 16, NTOK], BF16)
        kT2 = msb("kT2", [128, 4, 2048], BF16)
        vA = msb("vA", [128, 16, 4, 65], BF16)
        esink = msb("esink", [128, 32])
        hT = sb("hT1", [128, 16, NTOK], BF16)
        xbufs = Rot([sb(f"xb1{i}", [128, D]) for i in range(1)])
        prologue_hT(1, Xs, hT, xbufs, Rot([PS[0], PS[1], PS[2], PS[3]]))
        P.memset("dve", vA[:, :, :, 64:65], 1.0)
        bcast_load(esink[:], sink[0:1, :])
        P.act(esink[:], esink[:], AF.Exp)
        ropeC = sb("ropeC1", [128, 1024])
        ropeS = sb("ropeS1", [128, 1024])
        swp = sb("swp1", [128, 128])
        P.dma("sp", ropeC[:], c_ropeC)
        P.dma("sp", ropeS[:], c_ropeS)
        P.dma("sp", swp[:], c_swap)
        cCt = sb("cCt", [128, 8, 32])
        cSt = sb("cSt", [128, 8, 32])
        P.dma("sp", cCt[:], c_ropeCt)
        P.dma("sp", cSt[:], c_ropeSt)
        wpan = Rot([sb(f"wpan1{i}", [128, 16, 512], BF16) for i in range(2)])
        q32 = Rot([sb(f"q32b{i}", [128, 512]) for i in range(2)])
        r32 = sb("r32c", [128, 512])
        r32b = sb("r32d", [128, 512])
        for pn in range(4):
            wb = wpan.get()
            load_panel(wb, w_in_odd[:, pn * 512:(pn + 1) * 512], 16, 512)
            for f in range(4):
                ch = pn * 4 + f
                for tb in range(3):
                    ps = PS[(f * 3 + tb) % 3]
                    for c in range(16):
                        P.mm(ps[:, :], wb[:, c, f * 128:(f + 1) * 128], hT[:, c, tb * 512:(tb + 1) * 512], c == 0, c == 15)
                    if tb < 2:
                        q3 = q32.get()
                        P.copy("act", q3[:], ps[:, :])
                        ps2 = PS[3 + (tb % 2)]
                        P.mm(ps2[:, :], swp[:], q3[:], True, True)
                        P.tt("dve", r32b[:], ps2[:, :], ropeS[:, tb * 512:(tb + 1) * 512], ALU.mult)
                        P.tt("pool", r32[:], q3[:], ropeC[:, tb * 512:(tb + 1) * 512], ALU.mult)
                        P.tt("dve", qT[:, ch, tb * 512:(tb + 1) * 512], r32[:], r32b[:], ALU.add)
                    else:
                        evac_copy(qT[:, ch, 1024:1536], ps[:, :])
        wb = wpan.get()
        load_panel(wb, w_in_odd[:, 2048:2560], 16, 512)
        kvt = Rot([sb(f"kvt{i}", [128, 512]) for i in range(2)])
        kd = Rot([sb(f"kd{i}", [128, 4, 128]) for i in range(2)])
        kr = sb("kr", [128, 4, 64])
        tA = sb("tA", [128, 4, 32])
        tB = sb("tB", [128, 4, 32])

        def k_to_T(ksrc, col0, ps):
            kdd = kd.get()
            P.copy("pool", kdd[:, :, 0:64], ksrc)
            P.copy("pool", kdd[:, :, 64:128], ksrc)
            for g in range(4):
                P.tr(ps[:, g * 128:(g + 1) * 128], kdd[:, g, :], ident[:])
            evac_copy(kT2[:, :, col0:col0 + 128], ps[:, :].rearrange("p (g t) -> p g t", g=4))

        for t in range(NT):
            ps = PS[5 + (t % 2)]
            for c in range(16):
                P.mm(ps[:, :], hT[:, c, t * 128:(t + 1) * 128], wb[:, c, :], c == 0, c == 15)
            kv = kvt.get()
            evac_copy(kv[:], ps[:, :])
            if t >= 8:
                P.dma("sp", o_k[(t - 8) * 128:(t - 7) * 128, :], kv[:, 0:256])
                P.dma("sp", o_v[(t - 8) * 128:(t - 7) * 128, :], kv[:, 256:512])
            P.copy("dve", vA[:, t, :, 0:64], kv[:, 256:512].rearrange("p (g d) -> p g d", g=4))
            k4 = kv[:, 0:256].rearrange("p (g d) -> p g d", g=4)
            if t < 8:
                ke = kv[:, 0:256].rearrange("p (g i two) -> p g i two", g=4, two=2)[:, :, :, 0]
                ko = kv[:, 0:256].rearrange("p (g i two) -> p g i two", g=4, two=2)[:, :, :, 1]
                cb_ = cCt[:, t:t + 1, :].to_broadcast([128, 4, 32])
                sb_ = cSt[:, t:t + 1, :].to_broadcast([128, 4, 32])
                kre = kr[:].rearrange("p g (i two) -> p g i two", two=2)[:, :, :, 0]
                kro = kr[:].rearrange("p g (i two) -> p g i two", two=2)[:, :, :, 1]
                P.tt("dve", tA[:], ke, cb_, ALU.mult)
                P.tt("dve", tB[:], ko, sb_, ALU.mult)
                P.tt("dve", kre, tA[:], tB[:], ALU.subtract)
                P.tt("dve", tA[:], ke, sb_, ALU.mult)
                P.tt("dve", tB[:], ko, cb_, ALU.mult)
                P.tt("dve", kro, tA[:], tB[:], ALU.add)
                k_to_T(kr[:], t * 128, PS[3 + (t % 2)])
            else:
                k_to_T(k4, t * 128, PS[3 + (t % 2)])
        cbk = Rot([sb(f"cbk{i}", [128, 512]) for i in range(2)])
        for k4i in range(4):
            cb = cbk.get()
            P.dma("sp", cb[:, 0:256], c_swk[k4i * 128:(k4i + 1) * 128, :])
            P.dma("sp", cb[:, 256:512], c_swv[k4i * 128:(k4i + 1) * 128, :])
            P.copy("dve", vA[:, 12 + k4i, :, 0:64], cb[:, 256:512].rearrange("p (g d) -> p g d", g=4))
            k_to_T(cb[:, 0:256].rearrange("p (g d) -> p g d", g=4), 1536 + k4i * 128, PS[3 + (k4i % 2)])
        if debug:
            dbg_dump("d_qT1", qT[:], [128, 16, NTOK], BF16)
            dbg_dump("d_kT2", kT2[:], [128, 4, 2048], BF16)
        end_phase()
        if stage <= 5:
            P.emit()
            return nc, dbg

        catT = msb("catT1", [128, 16, NTOK], BF16)
        mprev = sb("mprev", [128, 128], BF16)
        mnext = sb("mnext", [128, 128], BF16)
        m32 = sb("m32", [128, 128])
        P.dma("sp", m32[:], c_mprev)
        P.copy("dve", mprev[:], m32[:])
        P.dma("sp", m32[:], c_mnext)
        P.copy("dve", mnext[:], m32[:])
        pTb = Rot([sb(f"pTo{i}", [128, 2, 4, 128], BF16) for i in range(3)])
        atok = Rot([sb(f"atok{i}", [128, 32, 64]) for i in range(2)])
        den = sb("den", [128, 8])
        SC_SWA = 64.0 ** -0.5
        for si, (t0, t1) in enumerate(SEQS):
            for t in range(t0, t1):
                at = atok.get()
                if si == 0:
                    kl = [(j, (mprev if j == t - 1 else (mnext if j == t + 1 else None))) for j in (t - 1, t, t + 1) if 0 <= j < 8]
                    kl += [(12 + j, None) for j in range(4)]
                else:
                    kl = [(j, None) for j in range(t0, t1)]
                for g in range(4):
                    acc = [PS[4 + (g % 2) * 2], PS[5 + (g % 2) * 2]]
                    def pv_swa(ki, kt, p):
                        for hh in range(8):
                            m, par = hh // 2, hh % 2
                            ab = acc[hh // 4]
                            o = (hh % 4) * 65
                            P.mm(ab[:, o:o + 65], p[:, par, m, :], vA[:, kt, g, :], ki == 0 and hh % 4 == 0, ki == len(kl) - 1)
                    pend = None
                    for ki, (kt, msk) in enumerate(kl):
                        kcol = kt * 128 if kt < 12 else 1536 + (kt - 12) * 128
                        p = pTb.get()
                        for par in range(2):
                            pss = PS[(ki * 2 + par) % 4]
                            P.mm(pss[:, :], kT2[par * 64:(par + 1) * 64, g, kcol:kcol + 128],
                                 qT[par * 64:(par + 1) * 64, 4 * g:4 * g + 4, t * 128:(t + 1) * 128], True, True)
                            P.act(p[:, par, :, :], pss[:, :].rearrange("p (m q) -> p m q", m=4), AF.Exp, scale=SC_SWA)
                        if msk is not None:
                            pv = p[:].rearrange("p a m q -> p (a m) q")
                            P.tt("pool", pv, pv, msk[:, None, :].to_broadcast([128, 8, 128]), ALU.mult)
                        if pend is not None:
                            pv_swa(*pend)
                        pend = (ki, kt, p)
                    pv_swa(*pend)
                    for half in range(2):
                        ab = acc[half]
                        abv = ab[:, 0:260].rearrange("p (h d) -> p h d", d=65)
                        P.tt("dve", den[:, half * 4:(half + 1) * 4], abv[:, :, 64], esink[:, g * 8 + half * 4:g * 8 + half * 4 + 4], ALU.add)
                        P.add("dve", lambda e, half=half: e.reciprocal(den[:, half * 4:(half + 1) * 4], den[:, half * 4:(half + 1) * 4]),
                              [den[:, half * 4:(half + 1) * 4]], [den[:, half * 4:(half + 1) * 4]])
                        P.tt("dve", at[:, g * 8 + half * 4:g * 8 + half * 4 + 4, :], abv[:, :, 0:64],
                             den[:, half * 4:(half + 1) * 4].unsqueeze(2).to_broadcast([128, 4, 64]), ALU.mult)
                atf = at[:].rearrange("p h d -> p (h d)")
                for cg in range(4):
                    ps = PS[cg % 4]
                    for k in range(4):
                        c = cg * 4 + k
                        P.tr(ps[:, k * 128:(k + 1) * 128], atf[:, c * 128:(c + 1) * 128], ident[:])
                    evac_copy(catT[:, cg * 4:(cg + 1) * 4, t * 128:(t + 1) * 128], ps[:, :].rearrange("p (k t) -> p k t", k=4))
        if debug:
            dbg_dump("d_catT1", catT[:], [128, 16, NTOK], BF16)
        end_phase()
        if stage <= 6:
            P.emit()
            return nc, dbg

        catS = dscr("catS", [128, 16, NTOK], BF16)
        P.dma("sp", catS, catT[:])
        end_phase()
        mixer[0].close()
        catT2 = sb("catT2", [128, 16, NTOK], BF16)
        P.dma("sp", catT2[:], catS)

        def cat1(c, t):
            return catT2[:, c, t * 128:(t + 1) * 128]
        outproj_ln_router(1, w_out_odd, cat1, Xs, aff, "b")
        end_phase()
        moe(1, aff, y_out, "b")
    P.emit()
    return nc, dbg


def _consts():
    f32 = np.float32
    c = {}
    c["c_ident"] = np.eye(128, dtype=f32)
    n_freq = 16
    inv = (10000.0 ** (-np.arange(n_freq, dtype=f32) / f32(n_freq))).astype(f32)
    tok = np.arange(1024)
    row = (tok // 64).astype(f32)
    col = (tok % 64).astype(f32)
    ang = np.concatenate([row[:, None] * inv[None, :], col[:, None] * inv[None, :]], -1).astype(f32)
    cos = np.cos(ang).astype(f32)
    sin = np.sin(ang).astype(f32)
    C = np.zeros((128, 1024), f32)
    S = np.zeros((128, 1024), f32)
    for f in range(128):
        i = (f % 64) // 2
        C[f] = cos[:, i]
        S[f] = -sin[:, i] if f % 2 == 0 else sin[:, i]
    c["c_ropeC"] = C
    c["c_ropeS"] = S
    sw = np.zeros((128, 128), f32)
    for f in range(0, 128, 2):
        sw[f + 1, f] = 1.0
        sw[f, f + 1] = 1.0
    c["c_swap"] = sw
    c["c_ropeCt"] = np.ascontiguousarray(cos.reshape(8, 128, 32).transpose(1, 0, 2))
    c["c_ropeSt"] = np.ascontiguousarray(sin.reshape(8, 128, 32).transpose(1, 0, 2))
    inv_c = np.ones((4, 1584), f32)
    for g, w in enumerate((2, 4, 8, 16)):
        for (pc, n) in ((8, 1024), (1048, 256), (1320, 256)):
            t = np.arange(n)
            lo = np.clip(t - w // 2, 0, n)
            hi = np.clip(t + w // 2, 0, n)
            inv_c[g, pc:pc + n] = (1.0 / (hi - lo).astype(f32)).astype(f32)
    c["c_invcnt"] = inv_c
    k = np.arange(128)
    c["c_triu"] = (k[:, None] < k[None, :]).astype(f32)
    c["c_ones"] = np.ones((128, 128), f32)
    c["c_iotaf"] = np.tile((np.arange(160) - 32).astype(f32)[None, :], (128, 1))
    c["c_iotap"] = np.arange(128, dtype=f32)[:, None].copy()
    sel = np.zeros((16, 16, 128), f32)
    for e in range(16):
        sel[e, e, :] = 1.0
    c["c_sel"] = sel
    c["c_mprev"] = (k[:, None] >= k[None, :]).astype(f32)
    c["c_mnext"] = (k[:, None] <= k[None, :]).astype(f32)
    return c


def _core_inputs(i, inp, consts):
    f = np.ascontiguousarray
    m = {}
    m["x"] = f(np.concatenate([inp["x_sample"][i], inp["x_prompt"][2 * i], inp["x_prompt"][2 * i + 1]], 0))
    cond = np.stack([inp["c"][i], inp["c_ctx"]], 0)
    m["condT"] = f(cond.reshape(2, 16, 128).transpose(2, 1, 0))
    m["c_ckv"] = f(inp["cache_mla_ckv"][i, 0])
    m["c_kpe"] = f(inp["cache_mla_kpe"][i, 0])
    m["c_swk"] = f(inp["cache_swa_k"][i, 0].reshape(512, 256))
    m["c_swv"] = f(inp["cache_swa_v"][i, 0].reshape(512, 256))
    return m


_SHARED = None


def _shared_inputs(inp):
    f = np.ascontiguousarray
    s = {}
    s["w_ada"] = inp["w_ada"]
    s["b_ada"] = inp["b_ada"]
    s["w_in_even"] = inp["w_in_even"][0]
    s["q_norm"] = inp["q_norm"]
    s["kv_norm"] = inp["kv_norm"]
    s["w_uq"] = inp["w_uq"][0]
    s["w_ukv"] = inp["w_ukv"][0]
    s["w_pool"] = inp["w_pool"][0]
    s["pool_scaleT"] = f(inp["pool_scale"][0].reshape(8, 128).T)
    s["w_out_even"] = inp["w_out_even"][0]
    s["w_in_odd"] = inp["w_in_odd"][0]
    s["sink"] = inp["sink"]
    s["w_out_odd"] = inp["w_out_odd"][0]
    for k in ("ln1_g", "ln1_b", "ln2_g", "ln2_b", "w_router", "w_gate", "w_up", "w_down"):
        s[k] = inp[k]
    return {k: f(np.asarray(v, dtype=np.float32)) for k, v in s.items()}


def _used_names(nc):
    return None


def kernel(**inputs):
    inp = {k: np.asarray(v) for k, v in inputs.items()}
    res = build_program()
    nc = res[0]
    consts = _consts()
    shared = _shared_inputs(inp)
    in_maps = []
    for i in range(8):
        m = dict(shared)
        m.update(consts)
        m.update(_core_inputs(i, inp, consts))
        in_maps.append(m)
    r = run_bass_kernel_spmd(nc, in_maps, core_ids=list(range(8)))
    outs = r.results
    y_prompt = np.zeros((16, 256, 2048), np.float32)
    y_sample = np.zeros((8, 1024, 2048), np.float32)
    ckv = np.zeros((16, 1, 256, 256), np.float32)
    kpe = np.zeros((16, 1, 256, 64), np.float32)
    sk = np.zeros((16, 1, 256, 4, 64), np.float32)
    sv = np.zeros((16, 1, 256, 4, 64), np.float32)
    for i in range(8):
        o = outs[i]
        y = np.asarray(o["y"])
        y_sample[i] = y[0:1024]
        y_prompt[2 * i] = y[1024:1280]
        y_prompt[2 * i + 1] = y[1280:1536]
        for j in range(2):
            ckv[2 * i + j, 0] = np.asarray(o["o_ckv"])[j * 256:(j + 1) * 256]
            kpe[2 * i + j, 0] = np.asarray(o["o_kpe"])[j * 256:(j + 1) * 256]
            sk[2 * i + j, 0] = np.asarray(o["o_k"])[j * 256:(j + 1) * 256].reshape(256, 4, 64)
            sv[2 * i + j, 0] = np.asarray(o["o_v"])[j * 256:(j + 1) * 256].reshape(256, 4, 64)
    return (y_prompt, y_sample, ckv, kpe, sk, sv)
```
